# Optimizing a Trainium2 kernel written in Bass

```python
import math
import jax, jax.numpy as jnp
from jax import lax
import numpy as np

D_MODEL = 1024
BATCH = 8
SEQ = 4096
DEPTH = 1
DEC_BATCH = 1
DEC_SEQ = 16384
PAST_LEN = 128

N_HEADS = 8
HEAD_DIM = 64
D_ATTN = N_HEADS * HEAD_DIM
D_CONV = D_MODEL - D_ATTN
ROT_DIM = HEAD_DIM // 4
ROPE_THETA = 500000.0
CONV_WIDTH = 31
DILATED_PATTERNS = ((128, 1), (512, 4), (2048, 16))
ATTN_BLOCK = 64
NORM_EPS = 1e-6
LN_EPS = 1e-5
D_IN = 3 * D_ATTN + D_ATTN + 2 * D_CONV + D_CONV
NEG_INF = -1e30

kernel_name = "hymba_conformer_dilated_encoder"


def rms_norm(x, g):
    xf = x.astype(jnp.float32)
    y = xf * lax.rsqrt(jnp.mean(xf * xf, axis=-1, keepdims=True) + NORM_EPS)
    return (y * g.astype(jnp.float32)).astype(x.dtype)


def layer_norm(x, g, b):
    xf = x.astype(jnp.float32)
    mu = jnp.mean(xf, axis=-1, keepdims=True)
    xc = xf - mu
    var = jnp.mean(xc * xc, axis=-1, keepdims=True)
    y = xc * lax.rsqrt(var + LN_EPS) * g.astype(jnp.float32) + b.astype(jnp.float32)
    return y.astype(x.dtype)


def rope_partial(x):
    S = x.shape[1]
    half = ROT_DIM // 2
    inv = ROPE_THETA ** (-jnp.arange(half, dtype=jnp.float32) * 2.0 / ROT_DIM)
    ang = jnp.arange(S, dtype=jnp.float32)[:, None] * inv[None, :]
    cos = jnp.cos(ang)[:, None, :]
    sin = jnp.sin(ang)[:, None, :]
    xr = x[..., :ROT_DIM].astype(jnp.float32)
    x1, x2 = xr[..., :half], xr[..., half:]
    rot = jnp.concatenate([x1 * cos - x2 * sin, x2 * cos + x1 * sin], axis=-1)
    return jnp.concatenate([rot.astype(x.dtype), x[..., ROT_DIM:]], axis=-1)


def dilated_window_partial(q, k, v, window, dilation):
    B, S, H, Dh = q.shape
    d = dilation
    half = window // (2 * d)
    blk = ATTN_BLOCK
    assert half <= blk
    L = S // d
    nb = -(-L // blk)
    Lp = nb * blk

    def strided(t):
        return t.reshape(B, L, d, H, Dh).transpose(0, 2, 3, 1, 4)

    qs = jnp.pad(strided(q), ((0, 0), (0, 0), (0, 0), (0, Lp - L), (0, 0)))
    pad_k = ((0, 0), (0, 0), (0, 0), (blk, Lp - L + blk), (0, 0))
    ks = jnp.pad(strided(k), pad_k).reshape(B, d, H, nb + 2, blk, Dh)
    vs = jnp.pad(strided(v), pad_k).reshape(B, d, H, nb + 2, blk, Dh)
    qb = qs.reshape(B, d, H, nb, blk, Dh)
    kwin = jnp.concatenate([ks[:, :, :, :-2], ks[:, :, :, 1:-1], ks[:, :, :, 2:]], axis=-2)
    vwin = jnp.concatenate([vs[:, :, :, :-2], vs[:, :, :, 1:-1], vs[:, :, :, 2:]], axis=-2)

    qi = jnp.arange(nb)[:, None] * blk + jnp.arange(blk)[None, :]
    kj = (jnp.arange(nb)[:, None] - 1) * blk + jnp.arange(3 * blk)[None, :]
    rel = kj[:, None, :] - qi[:, :, None]
    valid = (jnp.abs(rel) <= half) & (kj >= 0)[:, None, :] & (kj < L)[:, None, :]

    scale = 1.0 / math.sqrt(Dh)
    s = jnp.einsum('bdhnqc,bdhnkc->bdhnqk', qb.astype(jnp.float32), kwin.astype(jnp.float32)) * scale
    s = jnp.where(valid, s, NEG_INF)
    m = jnp.max(s, axis=-1)
    p = jnp.where(valid, jnp.exp(s - m[..., None]), 0.0)
    den = jnp.sum(p, axis=-1)
    num = jnp.einsum('bdhnqk,bdhnkc->bdhnqc', p, vwin.astype(jnp.float32))

    num = num.reshape(B, d, H, Lp, Dh)[:, :, :, :L].transpose(0, 3, 1, 2, 4).reshape(B, S, H, Dh)
    m = m.reshape(B, d, H, Lp)[:, :, :, :L].transpose(0, 3, 1, 2).reshape(B, S, H)
    den = den.reshape(B, d, H, Lp)[:, :, :, :L].transpose(0, 3, 1, 2).reshape(B, S, H)
    return num, m, den


def dilated_mixture_attention(q, k, v):
    parts = [dilated_window_partial(q, k, v, w, d) for (w, d) in DILATED_PATTERNS]
    m_all = jnp.max(jnp.stack([p[1] for p in parts], axis=0), axis=0)
    num = sum(p[0] * jnp.exp(p[1] - m_all)[..., None] for p in parts)
    den = sum(p[2] * jnp.exp(p[1] - m_all) for p in parts)
    return (num / den[..., None]).astype(q.dtype)


def conformer_conv(a, b, conv_w, conv_b, ln_g, ln_b, w_pw, b_pw):
    u = a * jax.nn.sigmoid(b)
    pad = (CONV_WIDTH - 1) // 2
    u = lax.conv_general_dilated(u, conv_w[:, None, :].astype(u.dtype), window_strides=(1,),
                                 padding=[(pad, pad)], dimension_numbers=('NWC', 'WIO', 'NWC'),
                                 feature_group_count=D_CONV) + conv_b
    u = layer_norm(u, ln_g, ln_b)
    u = jax.nn.silu(u)
    return u @ w_pw + b_pw


def encoder_layer(x, norm_pre, w_in, conv_w, conv_b, conv_ln_g, conv_ln_b, w_pw, b_pw, w_out, norm_post):
    B, S, _ = x.shape
    h = rms_norm(x, norm_pre)
    z = h @ w_in
    cuts = np.cumsum([D_ATTN, D_ATTN, D_ATTN, D_ATTN, D_CONV, D_CONV])[:-0 or None]
    q, k, v, g_attn, c_a, c_b, g_conv = jnp.split(z, [int(c) for c in cuts[:6]], axis=-1)
    q = rope_partial(q.reshape(B, S, N_HEADS, HEAD_DIM))
    k = rope_partial(k.reshape(B, S, N_HEADS, HEAD_DIM))
    v = v.reshape(B, S, N_HEADS, HEAD_DIM)
    attn = dilated_mixture_attention(q, k, v).reshape(B, S, D_ATTN) * jax.nn.silu(g_attn)
    conv = conformer_conv(c_a, c_b, conv_w, conv_b, conv_ln_g, conv_ln_b, w_pw, b_pw) * jax.nn.silu(g_conv)
    y = jnp.concatenate([attn, conv], axis=-1) @ w_out
    return x + rms_norm(y, norm_post)


def setup_inputs(seed: int = 0) -> dict:
    key = jax.random.key(seed)
    ks = jax.random.split(key, 12)
    f32 = jnp.float32
    x_prompt = jax.random.normal(ks[0], (BATCH, SEQ, D_MODEL), f32)
    x_sample = jax.random.normal(ks[1], (DEC_BATCH, DEC_SEQ, D_MODEL), f32)
    norm_pre = 1.0 + 0.05 * jax.random.normal(ks[2], (DEPTH, D_MODEL), f32)
    w_in = jax.random.normal(ks[3], (DEPTH, D_MODEL, D_IN), f32) * D_MODEL ** -0.5
    conv_w = jax.random.normal(ks[4], (DEPTH, CONV_WIDTH, D_CONV), f32) * CONV_WIDTH ** -0.5
    conv_b = 0.02 * jax.random.normal(ks[5], (DEPTH, D_CONV), f32)
    conv_ln_g = 1.0 + 0.05 * jax.random.normal(ks[6], (DEPTH, D_CONV), f32)
    conv_ln_b = 0.02 * jax.random.normal(ks[7], (DEPTH, D_CONV), f32)
    w_pw = jax.random.normal(ks[8], (DEPTH, D_CONV, D_CONV), f32) * D_CONV ** -0.5
    b_pw = 0.02 * jax.random.normal(ks[9], (DEPTH, D_CONV), f32)
    w_out = jax.random.normal(ks[10], (DEPTH, D_MODEL, D_MODEL), f32) * D_MODEL ** -0.5
    norm_post = 1.0 + 0.05 * jax.random.normal(ks[11], (DEPTH, D_MODEL), f32)
    return {"x_prompt": x_prompt, "x_sample": x_sample, "norm_pre": norm_pre, "w_in": w_in,
            "conv_w": conv_w, "conv_b": conv_b, "conv_ln_g": conv_ln_g, "conv_ln_b": conv_ln_b,
            "w_pw": w_pw, "b_pw": b_pw, "w_out": w_out, "norm_post": norm_post}


def reference(x_prompt, x_sample, norm_pre, w_in, conv_w, conv_b, conv_ln_g, conv_ln_b, w_pw, b_pw, w_out, norm_post):
    y_prompt = x_prompt
    y_sample = x_sample
    for l in range(DEPTH):
        args = (norm_pre[l], w_in[l], conv_w[l], conv_b[l], conv_ln_g[l], conv_ln_b[l],
                w_pw[l], b_pw[l], w_out[l], norm_post[l])
        y_prompt = encoder_layer(y_prompt, *args)
        y_sample = encoder_layer(y_sample, *args)
    return (y_prompt, y_sample)
```

```python
import math
from contextlib import ExitStack

import numpy as np
import concourse.bass as bass
import concourse.mybir as mybir
from concourse.bass_utils import run_bass_kernel_spmd

F32 = mybir.dt.float32
BF16 = mybir.dt.bfloat16
I32 = mybir.dt.int32
AF = mybir.ActivationFunctionType
ALU = mybir.AluOpType

D = 1024
NQ = 2048
HALO = 1024
NE = NQ + 2 * HALO
NJOB = 3
NCORE = 8
PATTERNS = (16, 4, 1)
NEG = -30000.0
TWO_PI = 2.0 * math.pi

ENGS = ["sync", "scalar", "vector", "gpsimd", "tensor"]


def key_tile_list():
    out = []
    for d in PATTERNS:
        nt = NQ // (128 * d)
        for r in range(d):
            for i in range(nt + 1):
                out.append((d, r, i, nt))
    return out


def zero_window_tiles(job):
    if job == 0:
        return set(range(0, HALO // 128))
    if job == 1:
        return set(range((NE - HALO) // 128, NE // 128))
    return set()


KT_LIST = key_tile_list()
NKT = len(KT_LIST)


class Prog:
    def __init__(self, nc):
        self.nc = nc
        self.lists = {e: [] for e in ENGS}
        self.cnt = {e: 0 for e in ENGS}
        self.waited = {e: {} for e in ENGS}
        self.res = {}
        self.sems = {}
        self.dma_sems = {}
        self.dma_cnt = {}
        self.forced = {e: {} for e in ENGS}

    def _deps(self, eng, reads, writes):
        need = dict(self.forced[eng])
        self.forced[eng] = {}

        def add(k, v):
            if need.get(k, 0) < v:
                need[k] = v

        for r in reads:
            st = self.res.get(r)
            if st:
                for k, v in st["w"].items():
                    add(k, v)
        for w in writes:
            st = self.res.get(w)
            if st:
                for k, v in st["w"].items():
                    add(k, v)
                for k, v in st["r"].items():
                    add(k, v)
        waits = []
        for k, v in need.items():
            if k == "tensor" and eng == "tensor":
                continue
            if self.waited[eng].get(k, 0) < v:
                self.waited[eng][k] = v
                waits.append((k, v))
        return waits

    def _mark(self, reads, writes, tok):
        for r in reads:
            st = self.res.setdefault(r, {"w": {}, "r": {}})
            st["r"][tok[0]] = tok[1]
        for w in writes:
            st = self.res.setdefault(w, {"w": {}, "r": {}})
            st["w"][tok[0]] = tok[1]

    def op(self, eng, fn, reads=(), writes=()):
        waits = self._deps(eng, reads, writes)
        self.cnt[eng] += 1
        tok = (eng, self.cnt[eng])
        self.lists[eng].append(("op", waits, fn, None))
        self._mark(reads, writes, tok)
        return tok

    def dma(self, queue, out, in_, semkey, reads=(), writes=(), **kw):
        waits = self._deps(queue, reads, writes)
        self.dma_cnt[semkey] = self.dma_cnt.get(semkey, 0) + 16
        tok = (("dma", semkey), self.dma_cnt[semkey])
        self.lists[queue].append(("dma", waits, (out, in_, kw), semkey))
        self._mark(reads, writes, tok)
        return tok

    def barrier(self, dummy):
        waits = [(e, self.cnt[e]) for e in ENGS if e != "gpsimd" and self.cnt[e] > 0]
        waits += [(("dma", k), v) for k, v in self.dma_cnt.items()]
        waits.append(("gpsimd", self.cnt["gpsimd"]))
        w2 = []
        for k, v in waits:
            if v > 0 and self.waited["gpsimd"].get(k, 0) < v:
                self.waited["gpsimd"][k] = v
                w2.append((k, v))
        self.cnt["gpsimd"] += 1
        tok = ("gpsimd", self.cnt["gpsimd"])
        self.lists["gpsimd"].append(("op", w2, lambda e: e.memset(dummy, 0.0), None))
        for e in ENGS:
            self.forced[e]["gpsimd"] = tok[1]

    def wait_all(self, eng):
        waits = [(("dma", k), v) for k, v in self.dma_cnt.items()]
        waits += [(e, self.cnt[e]) for e in ENGS if self.cnt[e] > 0 and e != eng]
        self.lists[eng].append(("wait", waits, None, None))

    def emit(self, stack):
        nc = self.nc
        for e in ENGS:
            self.sems[e] = stack.enter_context(nc.semaphore("s_" + e))
        for k in self.dma_cnt:
            self.dma_sems[k] = stack.enter_context(nc.semaphore("d_" + str(k)))
        block = stack.enter_context(nc.Block())

        def semof(k):
            if isinstance(k, tuple):
                return self.dma_sems[k[1]]
            return self.sems[k]

        def run(ename):
            def body(engine):
                for kind, waits, payload, semkey in self.lists[ename]:
                    for k, v in waits:
                        engine.wait_ge(semof(k), v)
                    if kind == "op":
                        payload(engine).then_inc(self.sems[ename], 1)
                    elif kind == "dma":
                        out, in_, kw = payload
                        engine.dma_start(out=out, in_=in_, **kw).then_inc(self.dma_sems[semkey], 16)
            return body

        block.sync(run("sync"))
        block.scalar(run("scalar"))
        block.vector(run("vector"))
        block.gpsimd(run("gpsimd"))
        block.tensor(run("tensor"))


_LASTP = [None]


class _Stop(Exception):
    pass


def build_program(njob=NJOB, dbg=None, upto=None, ktlimit=None, sub=None):
    nc = bass.Bass("TRN2", target_bir_lowering=False)
    dt = nc.dram_tensor
    xw = dt("xw", [njob, NE, D], F32, kind="ExternalInput").ap()
    posd = dt("pos", [njob, NE], F32, kind="ExternalInput").ap()
    kvald = dt("kval", [njob, 128, NKT], F32, kind="ExternalInput").ap()
    cmatd = dt("cmat", [128, 640], F32, kind="ExternalInput").ap()
    prmd = dt("prm", [128, 160], F32, kind="ExternalInput").ap()
    gpostd = dt("gpost", [128, D], F32, kind="ExternalInput").ap()
    wcvd = dt("wcv", [4, 128, 8, 384], F32, kind="ExternalInput").ap()
    whpd = dt("whp", [4, 128, 8, 512], F32, kind="ExternalInput").ap()
    wpwd = dt("wpw", [128, 4, 512], F32, kind="ExternalInput").ap()
    woutd = dt("wout", [128, 8, 1024], F32, kind="ExternalInput").ap()
    yd = dt("y", [njob, NQ, D], F32, kind="ExternalOutput").ap()
    dbg_out = {}
    if dbg:
        for name, shape in dbg.items():
            dbg_out[name] = dt("dbg_" + name, list(shape), F32, kind="ExternalOutput").ap()

    st = ExitStack()
    with st:
        sb = lambda name, shape, dtype: st.enter_context(nc.sbuf_tensor("s_" + name, shape, dtype))
        hT = sb("hT", [128, 8, NE], BF16)
        catT2 = sb("catT", [128, 8 * NQ], BF16)
        catT = catT2[:, :].rearrange("p (k n) -> p k n", k=8)
        XR = sb("XR", [128, 12288], F32)
        ropeC = sb("ropeC", [128, NE], BF16)
        ropeS = sb("ropeS", [128, NE], BF16)
        wst = [sb("wst%d" % i, [128, 2, 512], F32) for i in range(2)]
        wblk = [sb("wblk%d" % i, [128, 8, 512], BF16) for i in range(2)]
        wpw = sb("wpw", [128, 4, 512], BF16)
        cb = sb("cb", [128, 640], BF16)
        prm = sb("prm", [128, 160], F32)
        vbt = sb("vbt", [128, NKT], F32)
        small = sb("small", [128, 16], F32)
        small2 = sb("small2", [128, 32], F32)
        SCR = sb("SCR", [128, 4096], F32)
        PSALL = st.enter_context(nc.psum_tensor("psall", [128, 4096], F32))
        PB = [PSALL[:, 512 * i:512 * (i + 1)] for i in range(8)]
        PT16 = PSALL[:, 3584:4096].bitcast(BF16)

        P = Prog(nc)
        _LASTP[0] = P
        V, A, G, T, S = "vector", "scalar", "gpsimd", "tensor", "sync"
        cf = XR[:, 8192:8832]

        ident_b = cb[:, 0:128]
        perm_b = cb[:, 128:256]
        band_b = cb[:, 256:512]
        onesm_b = cb[:, 512:640]
        gpre = prm[:, 0:8]
        invf = prm[:, 148:149]
        sgn = prm[:, 149:150]
        eps_rms = small[:, 0:1]
        eps_ln = small[:, 1:2]
        dummy = small[:, 8:9]

        P.dma(S, cf[:], cmatd[:, :], "c0", writes=["cf"])
        P.dma(S, prm[:], prmd[:, :], "c1", writes=["prm"])
        P.op(V, lambda e: e.tensor_copy(out=cb[:], in_=cf[:]), reads=["cf"], writes=["cb"])
        P.op(G, lambda e: e.memset(small[:, 0:1], 1e-6), writes=["small"])
        P.op(G, lambda e: e.memset(small[:, 1:2], 1e-5), writes=["small"])
        for hlf in range(2):
            P.dma(S, wst[hlf][:], wpwd[:, 2 * hlf:2 * hlf + 2, :], "wst%d" % hlf, writes=[("wst", hlf)])
            P.op(G, (lambda h: lambda e: e.tensor_copy(out=wpw[:, 2 * h:2 * h + 2, :], in_=wst[h][:]))(hlf),
                 reads=[("wst", hlf)], writes=["wpw"])

        wst_ctr = [0]

        def load_wblock(src, slot, W, gain=True):
            for q4 in range(4):
                s = wst_ctr[0] % 2
                wst_ctr[0] += 1
                P.dma(S, wst[s][:, :, 0:W], src[:, 2 * q4:2 * q4 + 2, :], "wst%d" % s, writes=[("wst", s)])
                if gain:
                    P.op(V, (lambda s, q4: lambda e: e.tensor_tensor(
                        out=wblk[slot][:, 2 * q4:2 * q4 + 2, 0:W], in0=wst[s][:, :, 0:W],
                        in1=gpre[:, 2 * q4:2 * q4 + 2].unsqueeze(2).to_broadcast([128, 2, W]), op=ALU.mult))(s, q4),
                        reads=[("wst", s), "prm"], writes=[("wblk", slot)])
                else:
                    P.op(V, (lambda s, q4: lambda e: e.tensor_copy(
                        out=wblk[slot][:, 2 * q4:2 * q4 + 2, 0:W], in_=wst[s][:, :, 0:W]))(s, q4),
                        reads=[("wst", s)], writes=[("wblk", slot)])

        def proj(ps, slot, c0, t0, n, first_reads=()):
            hkeys = [("hT", tt) for tt in range(t0 // 128, (t0 + n - 1) // 128 + 1)]
            for kc in range(8):
                P.op(T, (lambda kc: lambda e: e.matmul(ps, lhsT=wblk[slot][:, kc, c0:c0 + 128], rhs=hT[:, kc, t0:t0 + n],
                                                      start=(kc == 0), stop=(kc == 7)))(kc),
                     reads=[("wblk", slot)] + hkeys, writes=[ps_key(ps)])

        psk = {}

        def ps_key(ap):
            return psk[id(ap)]

        def pv(bank, a, b, key=None):
            ap = PB[bank][:, a:b]
            psk[id(ap)] = key if key is not None else ("pb", bank)
            return ap

        def dump(name, ap_src, rows, reads):
            if dbg and name in dbg_out:
                P.dma(S, dbg_out[name], ap_src, "dbg", reads=reads)

        def stage_end(name):
            if upto == name:
                raise _Stop()

        PIS = 3.1415925
        PT16s = [PB[7 - 0] if False else PSALL[:, 3584:4096].bitcast(BF16), PSALL[:, 2048:2560].bitcast(BF16)]
        PT16k = [("pb", 7), ("pb", 4)]
        PT16b6 = PSALL[:, 3072:3584].bitcast(BF16)
        def make_A(job):
            xt = [SCR[:, 0:1024], SCR[:, 1024:2048], SCR[:, 2048:3072]]
            hb = [SCR[:, 3072:3584].bitcast(BF16), SCR[:, 3584:4096].bitcast(BF16)]
            ptA = [PT16s[0], PT16b6]
            ptAk = [("pb", 7), ("pb", 6)]
            NT = NE // 128

            acnt = {}

            def a1_dma(t):
                acnt[t] = len(acnt)
                x3 = acnt[t] % 3
                P.dma(S, xt[x3], xw[job, t * 128:(t + 1) * 128, :], "x%d" % x3, writes=[("xt", x3)])

            def a1_norm(t):
                s = acnt[t] % 2
                x3 = acnt[t] % 3
                ss = small[:, 2 + s:3 + s]
                P.op(A, lambda e: e.activation(out=hb[s], in_=xt[x3], func=AF.Square, accum_out=ss), reads=[("xt", x3)], writes=[("hb", s), ("ss", s)])
                P.op(A, lambda e: e.activation(out=ss, in_=ss, func=AF.Sqrt, bias=eps_rms, scale=1.0 / D), reads=[("ss", s), "small"], writes=[("ss", s)])
                P.op(V, lambda e: e.reciprocal(out=ss, in_=ss), reads=[("ss", s)], writes=[("ss", s)])
                P.op(V, lambda e: e.tensor_scalar(out=hb[s], in0=xt[x3], scalar1=ss, scalar2=None, op0=ALU.mult),
                     reads=[("xt", x3), ("ss", s)], writes=[("hb", s)])

            def a1(t):
                a1_dma(t)
                a1_norm(t)


            def a2(t):
                s = acnt[t] % 2
                for kc in range(8):
                    P.op(T, (lambda kc: lambda e: e.transpose(ptA[s][:, kc * 128:(kc + 1) * 128], hb[s][:, kc * 128:(kc + 1) * 128], ident_b))(kc),
                         reads=[("hb", s), "cb"], writes=[ptAk[s]])
                P.op(V, lambda e: e.tensor_copy(out=hT[:, :, t * 128:(t + 1) * 128], in_=ptA[s].rearrange("p (k n) -> p k n", n=128)),
                     reads=[ptAk[s]], writes=[("hT", t)])

            zt = zero_window_tiles(job)
            first_tiles = [t for t in range(7, 25) if t not in zt]
            deferred = [t for t in range(NT) if t not in first_tiles and t not in zt]
            for t in sorted(zt):
                if 7 <= t < 25:
                    P.op(G, (lambda t: lambda e: e.memset(hT[:, :, t * 128:(t + 1) * 128], 0.0))(t), writes=[("hT", t)])
            RPN = 512
            rbase = 9216
            rsets = [(XR[:, rbase + 3 * RPN * i:rbase + 3 * RPN * i + RPN], XR[:, rbase + 3 * RPN * i + RPN:rbase + 3 * RPN * i + 2 * RPN],
                      XR[:, rbase + 3 * RPN * i + 2 * RPN:rbase + 3 * RPN * i + 3 * RPN]) for i in range(2)]
            halfpi = small[:, 9:10]
            P.op(G, lambda e: e.memset(halfpi, 0.5 * math.pi), writes=["small"])

            def rope_a(k):
                rp_pos, rp_s, rp_a = rsets[k % 2]
                kp, ks, ka = ("posf", k % 2), ("s1", k % 2), ("ra", k % 2)
                rp_si = rp_s.bitcast(I32)
                c0 = RPN * k
                P.dma(S, rp_pos, posd[job, c0:c0 + RPN].partition_broadcast(128), "r%d" % (k % 2), writes=[kp])
                P.op(V, lambda e: e.tensor_scalar(out=rp_pos, in0=rp_pos, scalar1=prm[:, 150:151], scalar2=None, op0=ALU.mult), reads=[kp, "prm"], writes=[kp])
                P.op(V, lambda e: e.tensor_copy(out=rp_si, in_=rp_pos), reads=[kp], writes=[ks, ka])
                P.op(V, lambda e: e.tensor_copy(out=rp_s, in_=rp_si), reads=[ks], writes=[ks])
                P.op(V, lambda e: e.tensor_tensor(out=rp_s, in0=rp_pos, in1=rp_s, op=ALU.subtract), reads=[ks, kp], writes=[ks])
                P.op(V, lambda e: e.tensor_scalar(out=rp_s, in0=rp_s, scalar1=-0.4999999, scalar2=0.4999999, op0=ALU.max, op1=ALU.min), reads=[ks], writes=[ks])

            def rope_b(k):
                rp_pos, rp_s, rp_a = rsets[k % 2]
                ks, ka = ("s1", k % 2), ("ra", k % 2)
                c0 = RPN * k
                P.op(A, lambda e: e.activation(out=rp_a, in_=rp_s, func=AF.Sin, scale=math.pi), reads=[ks], writes=[ka])
                P.op(A, lambda e: e.activation(out=ropeS[:, c0:c0 + RPN], in_=rp_s, func=AF.Sin, scale=prm[:, 151:152]), reads=[ks, "prm"], writes=["ropeS"])

            def rope_c(k):
                rp_pos, rp_s, rp_a = rsets[k % 2]
                ks, ka = ("s1", k % 2), ("ra", k % 2)
                c0 = RPN * k
                P.op(V, lambda e: e.tensor_tensor(out=rp_a, in0=rp_a, in1=rp_a, op=ALU.mult), reads=[ka], writes=[ka])
                P.op(V, lambda e: e.tensor_scalar(out=ropeC[:, c0:c0 + RPN], in0=rp_a, scalar1=-2.0, scalar2=1.0, op0=ALU.mult, op1=ALU.add),
                     reads=[ka], writes=["ropeC"])

            NRP = NE // RPN
            rstep = [0]

            def rope_step():
                r = rstep[0]
                rstep[0] += 1
                if 0 <= r - 2 < NRP:
                    rope_c(r - 2)
                if 0 <= r - 1 < NRP:
                    rope_b(r - 1)
                if r < NRP:
                    rope_a(r)

            steps = []

            def mk(n_, t):
                def f():
                    if t is not None:
                        a1(t)
                    if n_ >= 1:
                        a2(first_tiles[n_ - 1])
                    pass
                return f

            for n_, t in enumerate(first_tiles + [None]):
                steps.append(mk(n_, t))
            return dict(steps=steps, a1_dma=a1_dma, a1_norm=a1_norm, a2=a2, deferred=deferred)

        def make_E(job):
            gpb = XR[:, 0:1024]
            xt2s = [XR[:, 1024 * (1 + i):1024 * (2 + i)] for i in range(4)]
            ots = [XR[:, 5120:6144], XR[:, 6144:7168]]
            P.dma(S, gpb, gpostd[:, :], "c3", writes=["gpb"])
            junk2s = [XR[:, 7168:7680].bitcast(BF16), XR[:, 7680:8192].bitcast(BF16)]
            pys = {}
            ssE0 = small2[:, 0:16]
            ssE1 = small2[:, 16:32]

            def e1(t):
                s = t % 2
                x4 = t % 4
                P.dma(S, xt2s[x4], xw[job, HALO + t * 128:HALO + (t + 1) * 128, :], "xe%d" % x4, writes=[("xt2", x4)])
                b4 = 2 * (t % 3)
                py = [pv(b4, 0, 512), pv(b4 + 1, 0, 512)]
                pys[t] = py
                for nh in range(2):
                    for kc in range(8):
                        P.op(T, (lambda nh, kc: lambda e: e.matmul(py[nh], lhsT=catT[:, kc, t * 128:(t + 1) * 128], rhs=wblk[nh][:, kc, :],
                                                                   start=(kc == 0), stop=(kc == 7)))(nh, kc),
                             reads=["catT", ("wblk", nh)], writes=[("pb", b4 + nh)])
                P.op(A, lambda e: e.activation(out=junk2s[s][:, 0:1024], in_=PSALL[:, 512 * b4:512 * (b4 + 2)], func=AF.Square,
                                               accum_out=ssE0[:, t:t + 1]),
                     reads=[("pb", b4), ("pb", b4 + 1)], writes=[("junk2", s), ("ssE", t // 2)])

            def e2(t):
                s = t % 2
                py = pys[t]
                ot = ots[s]
                xt2 = xt2s[t % 4]
                k = ("ssE", t // 2)
                a0 = ssE0[:, t:t + 1]
                P.op(A, lambda e: e.activation(out=a0, in_=a0, func=AF.Sqrt, bias=eps_rms, scale=1.0 / D), reads=[k, "small"], writes=[k])
                P.op(V, lambda e: e.reciprocal(out=a0, in_=a0), reads=[k], writes=[k])
                for nh in range(2):
                    P.op(V, (lambda nh: lambda e: e.scalar_tensor_tensor(out=ot[:, nh * 512:(nh + 1) * 512], in0=py[nh], scalar=a0,
                                                                         in1=gpb[:, nh * 512:(nh + 1) * 512], op0=ALU.mult, op1=ALU.mult))(nh),
                         reads=[("pb", 2 * (t % 3) + nh), k, "gpb"], writes=[("ot", s)])
                P.op(V, lambda e: e.tensor_tensor(out=ot, in0=ot, in1=xt2, op=ALU.add), reads=[("ot", s), ("xt2", t % 4)], writes=[("ot", s)])
                P.dma(G, yd[job, t * 128:(t + 1) * 128, :], ot, "st%d" % s, reads=[("ot", s)])


            NTE = NQ // 128
            steps = []

            def mk(t):
                def f():
                    if t < NTE:
                        e1(t)
                    if t >= 1:
                        e2(t - 1)
                return f

            for t in range(NTE + 1):
                steps.append(mk(t))
            return steps

        try:
          A_cur = make_A(0)
          for st_ in A_cur["steps"]:
              st_()
          for job in range(njob):
              a1_dma, a1_norm, a2, deferred = A_cur["a1_dma"], A_cur["a1_norm"], A_cur["a2"], A_cur["deferred"]
              load_wblock(wcvd[0], 0, 384)
              P.dma(S, vbt[:], kvald[job], "c2", writes=["vbt"])
              P.op(V, lambda e: e.tensor_scalar(out=vbt[:], in0=vbt[:], scalar1=-1.0, scalar2=-NEG, op0=ALU.add, op1=ALU.mult),
                   reads=["vbt"], writes=["vbt"])


              P.barrier(dummy)
              stage_end("A")

              uT = XR[:, 0:4160].bitcast(BF16).rearrange("p (g n) -> p g n", g=4)
              diag = XR[:, 4160:4160 + 7936].bitcast(BF16).rearrange("p (g j n) -> p g j n", g=4, j=31)
              cscr0 = catT2[:, 0:4 * NQ].bitcast(F32)
              sigts = [cscr0[:, 0:512], cscr0[:, 512:1024]]
              sigts2 = [cscr0[:, 1536:2048], cscr0[:, 2048:2560]]
              dq = list(deferred)
              dstep = [0]
              u0 = HALO - 16
              pieces = [(u0 + 512 * c, 512) for c in range(4)] + [(u0 + 2048, 32)]
              pn = 0
              for g in range(4):
                  slot = g % 2
                  if g < 3:
                      load_wblock(wcvd[g + 1], (g + 1) % 2, 384)
                  else:
                      load_wblock(whpd[0], 0, 512)
                  for pidx, (t0, n) in enumerate(pieces):
                      if pidx == 2:
                          P.op(V, (lambda gg: lambda e: e.tensor_tensor(
                              out=diag[:, gg, :, :], in0=ident_b.unsqueeze(1).to_broadcast([128, 31, 128]),
                              in1=prm[:, 24 + gg * 31:24 + gg * 31 + 31].unsqueeze(2).to_broadcast([128, 31, 128]), op=ALU.mult))(g),
                              reads=["cb", "prm"], writes=["diag"])
                      bs = 3 * (pn % 2)
                      sg_ = sigts[pn % 2]
                      sk = ("sigt", pn % 2)
                      pn += 1
                      pa = pv(bs + 0, 0, n)
                      pb_ = pv(bs + 1, 0, n)
                      proj(pa, slot, 0, t0, n)
                      proj(pb_, slot, 128, t0, n)
                      P.op(A, (lambda pb_, n, sg_: lambda e: e.activation(out=sg_[:, 0:n], in_=pb_, func=AF.Sigmoid))(pb_, n, sg_),
                           reads=[ps_key(pb_)], writes=[sk])
                      P.op(V, (lambda pa, n, g, t0, sg_: lambda e: e.tensor_tensor(out=uT[:, g, t0 - u0:t0 - u0 + n], in0=pa, in1=sg_[:, 0:n],
                                                                                    op=ALU.mult))(pa, n, g, t0, sg_),
                           reads=[ps_key(pa), sk], writes=["uT"])
                      if n == 512:
                          pg = pv(bs + 2, 0, 512)
                          proj(pg, slot, 256, t0 + 16, 512)
                          q0 = t0 + 16 - HALO
                          sg2 = sigts2[pn % 2]
                          sk2 = ("sigt2", pn % 2)
                          P.op(A, (lambda pg, sg2: lambda e: e.activation(out=sg2, in_=pg, func=AF.Sigmoid))(pg, sg2),
                               reads=[ps_key(pg)], writes=[sk2])
                          P.op(V, (lambda pg, g, q0, sg2: lambda e: e.tensor_tensor(out=catT[:, 4 + g, q0:q0 + 512], in0=pg, in1=sg2, op=ALU.mult))(pg, g, q0, sg2),
                               reads=[ps_key(pg), sk2], writes=["catT"])
                      nsteps = 1
                      for _ in range(nsteps):
                          kq = dstep[0]
                          dstep[0] += 1
                          if 0 <= kq - 2 < len(dq):
                              a2(dq[kq - 2])
                          if 0 <= kq - 1 < len(dq):
                              a1_norm(dq[kq - 1])
                          if kq < len(dq):
                              a1_dma(dq[kq])
              while dstep[0] - 2 < len(dq):
                  kq = dstep[0]
                  dstep[0] += 1
                  if 0 <= kq - 2 < len(dq):
                      a2(dq[kq - 2])
                  if 0 <= kq - 1 < len(dq):
                      a1_norm(dq[kq - 1])
                  if kq < len(dq):
                      a1_dma(dq[kq])
              P.barrier(dummy)
              stage_end("C1")
              CW = 512
              NCH = NQ // CW
              cscr = catT2[:, 0:4 * NQ].bitcast(F32)
              co = cscr[:, 0:2048].rearrange("p (g n) -> p g n", g=4)
              cob = cscr[:, 2048:3072].bitcast(BF16).rearrange("p (g n) -> p g n", g=4)
              sqb = cscr[:, 3072:4096].bitcast(BF16).rearrange("p (g n) -> p g n", g=4)
              slb = SCR[:, 0:1024].bitcast(BF16).rearrange("p (g n) -> p g n", g=4)
              m2 = SCR[:, 1024:1536]
              rstd = SCR[:, 1536:2048]
              nmr = SCR[:, 2048:2560]
              pcs = [pv(g, 0, CW) for g in range(4)]
              pm = pv(4, 0, CW)
              pq = pv(5, 0, CW)
              pws = [pv(6, 0, CW), pv(7, 0, CW)]

              def c2_conv(c, groups=(0, 1, 2, 3)):
                  q0 = c * CW
                  for g in groups:
                      for j in range(31):
                          off = q0 + 1 + j
                          P.op(T, (lambda g, j, off: lambda e: e.matmul(pcs[g], lhsT=diag[:, g, j, :], rhs=uT[:, g, off:off + CW],
                                                                        start=(j == 0), stop=(j == 30)))(g, j, off),
                               reads=["diag", "uT"], writes=[("pb", g)])

              def c2_evac(c):
                  for g in range(4):
                      bia = prm[:, 8 + g:9 + g]
                      P.op(A, (lambda g, bia: lambda e: e.activation(out=co[:, g, :], in_=pcs[g], func=AF.Identity, bias=bia, scale=1.0))(g, bia),
                           reads=[("pb", g), "prm"], writes=[("co", g)])
                      P.op(A, (lambda g, bia: lambda e: e.activation(out=cob[:, g, :], in_=pcs[g], func=AF.Identity, bias=bia, scale=1.0))(g, bia),
                           reads=[("pb", g), "prm"], writes=[("cob", g)])
                      P.op(A, (lambda g, bia: lambda e: e.activation(out=sqb[:, g, :], in_=pcs[g], func=AF.Square, bias=bia, scale=1.0))(g, bia),
                           reads=[("pb", g), "prm"], writes=[("sqb", g)])

              def c2_stats(c):
                  for g in range(4):
                      P.op(T, (lambda g: lambda e: e.matmul(pm, lhsT=onesm_b, rhs=cob[:, g, :], start=(g == 0), stop=(g == 3)))(g),
                           reads=["cb", ("cob", g)], writes=[("pb", 4)])
                  for g in range(4):
                      P.op(T, (lambda g: lambda e: e.matmul(pq, lhsT=onesm_b, rhs=sqb[:, g, :], start=(g == 0), stop=(g == 3)))(g),
                           reads=["cb", ("sqb", g)], writes=[("pb", 5)])

              def c2_rest(c):
                  q0 = c * CW
                  P.op(A, lambda e: e.activation(out=m2, in_=pm, func=AF.Square), reads=[("pb", 4)], writes=["m2"])
                  P.op(V, lambda e: e.tensor_tensor(out=m2, in0=pq, in1=m2, op=ALU.subtract), reads=[("pb", 5), "m2"], writes=["m2"])
                  P.op(A, lambda e: e.activation(out=rstd, in_=m2, func=AF.Sqrt, bias=eps_ln, scale=1.0), reads=["m2", "small"], writes=["rstd"])
                  P.op(V, lambda e: e.reciprocal(out=rstd, in_=rstd), reads=["rstd"], writes=["rstd"])
                  P.op(V, lambda e: e.scalar_tensor_tensor(out=nmr, in0=pm, scalar=-1.0, in1=rstd, op0=ALU.mult, op1=ALU.mult),
                       reads=[("pb", 4), "rstd"], writes=["nmr"])
                  for g in range(4):
                      P.op(V, (lambda g: lambda e: e.tensor_tensor(out=co[:, g, :], in0=co[:, g, :], in1=rstd, op=ALU.mult))(g),
                           reads=[("co", g), "rstd"], writes=[("co", g)])
                      P.op(V, (lambda g: lambda e: e.tensor_tensor(out=co[:, g, :], in0=co[:, g, :], in1=nmr, op=ALU.add))(g),
                           reads=[("co", g), "nmr"], writes=[("co", g)])
                      P.op(A, (lambda g: lambda e: e.activation(out=slb[:, g, :], in_=co[:, g, :], func=AF.Silu,
                                                                bias=prm[:, 16 + g:17 + g], scale=prm[:, 12 + g:13 + g]))(g),
                           reads=[("co", g), "prm"], writes=[("slb", g)])
                  for go in range(4):
                      pw_ = pws[go % 2]
                      for gi in range(4):
                          P.op(T, (lambda pw_, go, gi: lambda e: e.matmul(pw_, lhsT=wpw[:, gi, go * 128:(go + 1) * 128], rhs=slb[:, gi, :],
                                                                          start=(gi == 0), stop=(gi == 3)))(pw_, go, gi),
                               reads=["wpw", ("slb", gi)], writes=[("pb", 6 + go % 2)])
                      P.op(V, (lambda pw_, go, q0: lambda e: e.scalar_tensor_tensor(
                          out=catT[:, 4 + go, q0:q0 + CW], in0=pw_, scalar=prm[:, 20 + go:21 + go], in1=catT[:, 4 + go, q0:q0 + CW],
                          op0=ALU.add, op1=ALU.mult))(pw_, go, q0),
                          reads=[("pb", 6 + go % 2), "prm", "catT"], writes=["catT"])

              RP2 = 512
              q_pos, q_s, q_a = SCR[:, 2560:3072], SCR[:, 3072:3584], SCR[:, 3584:4096]
              q_si = q_s.bitcast(I32)

              def rp_dma(k):
                  P.dma(S, q_pos, posd[job, RP2 * k:RP2 * (k + 1)].partition_broadcast(128), "r0", writes=["qpos"])

              def rp_a(k):
                  P.op(V, lambda e: e.tensor_scalar(out=q_pos, in0=q_pos, scalar1=prm[:, 150:151], scalar2=None, op0=ALU.mult), reads=["qpos", "prm"], writes=["qpos"])
                  P.op(V, lambda e: e.tensor_copy(out=q_si, in_=q_pos), reads=["qpos"], writes=["qs", "qa"])
                  P.op(V, lambda e: e.tensor_copy(out=q_s, in_=q_si), reads=["qs"], writes=["qs"])
                  P.op(V, lambda e: e.tensor_tensor(out=q_s, in0=q_pos, in1=q_s, op=ALU.subtract), reads=["qs", "qpos"], writes=["qs"])
                  P.op(V, lambda e: e.tensor_scalar(out=q_s, in0=q_s, scalar1=-0.4999999, scalar2=0.4999999, op0=ALU.max, op1=ALU.min), reads=["qs"], writes=["qs"])
                  if k + 1 < NE // RP2:
                      rp_dma(k + 1)

              def rp_b(k):
                  c0 = RP2 * k
                  P.op(A, lambda e: e.activation(out=q_a, in_=q_s, func=AF.Sin, scale=math.pi), reads=["qs"], writes=["qa"])
                  P.op(A, lambda e: e.activation(out=ropeS[:, c0:c0 + RP2], in_=q_s, func=AF.Sin, scale=prm[:, 151:152]), reads=["qs", "prm"], writes=["ropeS"])

              def rp_c(k):
                  c0 = RP2 * k
                  P.op(V, lambda e: e.tensor_tensor(out=q_a, in0=q_a, in1=q_a, op=ALU.mult), reads=["qa"], writes=["qa"])
                  P.op(V, lambda e: e.tensor_scalar(out=ropeC[:, c0:c0 + RP2], in0=q_a, scalar1=-2.0, scalar2=1.0, op0=ALU.mult, op1=ALU.add),
                       reads=["qa"], writes=["ropeC"])

              rp_dma(0)
              c2_conv(0)
              for c in range(NCH):
                  c2_evac(c)
                  rp_a(2 * c)
                  if c + 1 < NCH:
                      c2_conv(c + 1, groups=(0,))
                  rp_b(2 * c)
                  c2_stats(c)
                  rp_c(2 * c)
                  rp_a(2 * c + 1)
                  if c + 1 < NCH:
                      c2_conv(c + 1, groups=(1, 2, 3))
                  rp_b(2 * c + 1)
                  c2_rest(c)
                  rp_c(2 * c + 1)
              c2keys = ["uT", "diag", "m2", "rstd", "nmr", "qpos", "qs", "qa"] + [(nm, g) for nm in ("co", "cob", "sqb", "slb") for g in range(4)]
              dkeys = ["kq0", "kq1", "vT", "sgT", "accA", "accB", "Dt", "catT", ("PT", 0), ("PT", 1), ("Vt", 0), ("Vt", 1), ("Vt", 2),
                       ("kraw", 0), ("kraw", 1), ("t1", 0), ("t1", 1)]
              P.op(G, lambda e: e.memset(dummy, 0.0), reads=[], writes=c2keys + dkeys)
              stage_end("C2")

              XB = XR[:, 0:6144].bitcast(BF16)
              kT = XB[:, 0:4096]
              vT = XB[:, 4096:8192]
              qT = XB[:, 8192:10240]
              sgT = XB[:, 10240:12288]
              accA = XR[:, 6144:8192]
              accB = XR[:, 8192:10240]
              Dt = XR[:, 10240:12288]
              kraws = [SCR[:, 2560:2816].bitcast(BF16), SCR[:, 2816:3072].bitcast(BF16)]
              t1s = [SCR[:, 3072:3584], SCR[:, 3584:4096]]
              PTt = [SCR[:, 0:256].bitcast(BF16), SCR[:, 256:512].bitcast(BF16)]
              Vt = [SCR[:, 512 + 96 * i:512 + 96 * (i + 1)].bitcast(BF16) for i in range(3)]
              for i in range(3):
                  P.op(G, (lambda i: lambda e: e.memset(Vt[i][:, 64:128], 1.0))(i), writes=[("Vt", i)])
              accbanks = [0, 1, 4]
              cn = [0]
              for hp in range(4):
                  slot = hp % 2
                  if hp < 3:
                      load_wblock(whpd[hp + 1], (hp + 1) % 2, 512)
                  else:
                      load_wblock(woutd[:, :, 0:512], 0, 512, gain=False)
                  pending = []

                  def rope_unit(u):
                      pp, kraw, t1, kk, tk, t0, dst, ch, which = u
                      P.op(T, lambda e: e.matmul(pp, lhsT=perm_b, rhs=kraw, start=True, stop=True), reads=["cb", kk], writes=[ps_key(pp)])
                      P.op(V, lambda e: e.tensor_tensor(out=t1, in0=pp, in1=ropeS[:, t0:t0 + 512], op=ALU.mult), reads=[ps_key(pp), "ropeS"], writes=[tk])
                      P.op(V, lambda e: e.tensor_tensor(out=kraw, in0=kraw, in1=ropeC[:, t0:t0 + 512], op=ALU.mult), reads=[kk, "ropeC"], writes=[kk])
                      P.op(V, lambda e: e.tensor_tensor(out=dst[:, ch * 512:(ch + 1) * 512], in0=kraw, in1=t1, op=ALU.add),
                           reads=[kk, tk], writes=["kq%d" % which])

                  zc = set(t // 4 for t in zero_window_tiles(job))
                  for ch in sorted(zc):
                      P.op(G, (lambda ch: lambda e: e.memset(kT[:, ch * 512:(ch + 1) * 512], 0.0))(ch), writes=["kq0"])
                      P.op(G, (lambda ch: lambda e: e.memset(vT[:, ch * 512:(ch + 1) * 512], 0.0))(ch), writes=["vT"])
                  for which, (c0, tbase, nch, dst) in enumerate(((128, 0, NE // 512, kT), (0, HALO, NQ // 512, qT))):
                      for ch in range(nch):
                          if which == 0 and ch in zc:
                              continue
                          t0 = tbase + ch * 512
                          n_ = cn[0]
                          cn[0] += 1
                          kraw = kraws[n_ % 2]
                          t1 = t1s[n_ % 2]
                          kk = ("kraw", n_ % 2)
                          tk = ("t1", n_ % 2)
                          pk = pv(accbanks[n_ % 3], 0, 512)
                          proj(pk, slot, c0, t0, 512)
                          P.op(A, (lambda pk, kraw: lambda e: e.activation(out=kraw, in_=pk, func=AF.Copy))(pk, kraw), reads=[ps_key(pk)], writes=[kk])
                          pp = pv(2 + n_ % 2, 0, 512)
                          if pending:
                              rope_unit(pending.pop(0))
                          pending.append((pp, kraw, t1, kk, tk, t0, dst, ch, which))
                  for ch in range(NE // 512):
                      if ch in zc:
                          continue
                      n_ = cn[0]
                      cn[0] += 1
                      pv_ = pv(accbanks[n_ % 3], 0, 512)
                      proj(pv_, slot, 256, ch * 512, 512)
                      if pending:
                          rope_unit(pending.pop(0))
                      if ch % 2 == 0:
                          P.op(A, (lambda pv_, ch: lambda e: e.activation(out=vT[:, ch * 512:(ch + 1) * 512], in_=pv_, func=AF.Copy))(pv_, ch),
                               reads=[ps_key(pv_)], writes=["vT"])
                      else:
                          P.op(V, (lambda pv_, ch: lambda e: e.tensor_copy(out=vT[:, ch * 512:(ch + 1) * 512], in_=pv_))(pv_, ch),
                               reads=[ps_key(pv_)], writes=["vT"])
                  for ch in range(NQ // 512):
                      n_ = cn[0]
                      cn[0] += 1
                      pg = pv(accbanks[n_ % 3], 0, 512)
                      proj(pg, slot, 384, HALO + ch * 512, 512)
                      P.op(A, (lambda pg, ch: lambda e: e.activation(out=sgT[:, ch * 512:(ch + 1) * 512], in_=pg, func=AF.Silu))(pg, ch),
                           reads=[ps_key(pg)], writes=["sgT"])
                  if hp == 3:
                      load_wblock(woutd[:, :, 512:1024], 1, 512, gain=False)
                  stage_end("Dp")

                  tasks = []
                  kt_base = 0
                  for pi_, d in enumerate(PATTERNS):
                      nt = NQ // (128 * d)
                      for r in range(d):
                          for i in range(nt + 1):
                              tasks.append(dict(d=d, r=r, i=i, nt=nt, kti=kt_base + r * (nt + 1) + i, first=(pi_ == 0)))
                      kt_base += d * (nt + 1)
                  if ktlimit is not None:
                      tasks = tasks[:ktlimit]

                  def geom(tk_):
                      d, r, i, nt = tk_["d"], tk_["r"], tk_["i"], tk_["nt"]
                      base = HALO // d
                      m0 = base + 128 * i
                      jlo = max(i - 1, 0)
                      jhi = min(i, nt - 1)
                      qa = (jlo - (i - 1)) * 128
                      qb = (jhi - (i - 1) + 1) * 128
                      mq0 = (m0 - 128 + qa) - base
                      return d, r, i, nt, m0, jlo, jhi, qa, qb, mq0

                  def ph1(n, tk_):
                      d, r, i, nt, m0, jlo, jhi, qa, qb, mq0 = geom(tk_)
                      vs = n % 3
                      kcols = kT.rearrange("p (m d) -> p m d", d=d)[:, m0 - 64:m0 + 64, r]
                      vcols = vT.rearrange("p (m d) -> p m d", d=d)[:, m0 - 64:m0 + 64, r]
                      qcols = qT.rearrange("p (m d) -> p m d", d=d)[:, mq0:mq0 + (qb - qa), r]
                      ptx = PT16s[n % 2]
                      ptk = PT16k[n % 2]
                      P.op(T, lambda e: e.transpose(ptx[:, 0:128], vcols, ident_b), reads=["vT", "cb"], writes=[ptk])
                      if n % 2 == 0:
                          P.op(A, lambda e: e.activation(out=Vt[vs].rearrange("p (a b) -> p a b", b=64)[:, 0:3:2, :],
                                                         in_=ptx[:, 0:128].rearrange("p (a b) -> p a b", b=64), func=AF.Copy),
                               reads=[ptk], writes=[("Vt", vs)])
                      else:
                          P.op(V, lambda e: e.tensor_copy(out=Vt[vs].rearrange("p (a b) -> p a b", b=64)[:, 0:3:2, :],
                                                          in_=ptx[:, 0:128].rearrange("p (a b) -> p a b", b=64)),
                               reads=[ptk], writes=[("Vt", vs)])
                      sbank = 2 * (n % 2)
                      for h in range(2):
                          psh = PB[sbank + h][:, qa:qb]
                          P.op(T, (lambda psh, h: lambda e: e.matmul(psh, lhsT=kcols[64 * h:64 * h + 64], rhs=qcols[64 * h:64 * h + 64],
                                                                     start=True, stop=True))(psh, h),
                               reads=["kq0", "kq1"], writes=[("pb", sbank + h)])

                  def ph2(n, tk_):
                      d, r, i, nt, m0, jlo, jhi, qa, qb, mq0 = geom(tk_)
                      kti = tk_["kti"]
                      sbank = 2 * (n % 2)
                      ptt = PTt[n % 2].rearrange("p (h q) -> p h q", h=2)
                      psv = PSALL[:, 512 * sbank:512 * (sbank + 2)].rearrange("p (h q) -> p h q", h=2)
                      P.op(A, lambda e: e.activation(out=ptt[:, :, qa:qb], in_=psv[:, :, qa:qb], func=AF.Exp, bias=vbt[:, kti:kti + 1], scale=0.125),
                           reads=[("pb", sbank), ("pb", sbank + 1), "vbt"], writes=[("PT", n % 2)])
                      P.op(V, lambda e: e.tensor_tensor(
                          out=ptt[:, :, qa:qb], in0=ptt[:, :, qa:qb],
                          in1=band_b[:, qa:qb].unsqueeze(1).to_broadcast([128, 2, qb - qa]), op=ALU.mult),
                          reads=[("PT", n % 2), "cb"], writes=[("PT", n % 2)])

                  def ph3(n, tk_):
                      d, r, i, nt, m0, jlo, jhi, qa, qb, mq0 = geom(tk_)
                      vs = n % 3
                      first = tk_["first"]
                      ptt = PTt[n % 2].rearrange("p (h q) -> p h q", h=2)
                      merged = (d != 16) and (jlo == i - 1) and (jhi == i) and (jlo % 4 != 3)
                      if merged:
                          sl0 = jlo % 4
                          for h in range(2):
                              po = PB[5 + h][:, sl0 * 128:(sl0 + 2) * 128]
                              lw = Vt[vs][:, 0:128] if h == 0 else Vt[vs][:, 64:192]
                              P.op(T, (lambda po, lw, h: lambda e: e.matmul(po, lhsT=lw, rhs=ptt[:, h, 0:256], start=False, stop=True,
                                                                            skip_group_check=True))(po, lw, h),
                                   reads=[("Vt", vs), ("PT", n % 2)], writes=[("po", h, sl0), ("po", h, sl0 + 1)])
                      for j in range(jlo, jhi + 1):
                          ca = (j - (i - 1)) * 128
                          sl = (r % 4) if d == 16 else (j % 4)
                          if not merged:
                              for h in range(2):
                                  po = PB[5 + h][:, sl * 128:(sl + 1) * 128]
                                  lw = Vt[vs][:, 0:128] if h == 0 else Vt[vs][:, 64:192]
                                  st_flag = (j == i) if d == 16 else (j == i and sl == 0)
                                  P.op(T, (lambda po, lw, h, ca, j, st_flag: lambda e: e.matmul(po, lhsT=lw, rhs=ptt[:, h, ca:ca + 128],
                                                                                                start=st_flag, stop=(j == i - 1),
                                                                                                skip_group_check=True))(po, lw, h, ca, j, st_flag),
                                       reads=[("Vt", vs), ("PT", n % 2)], writes=[("po", h, sl)])
                          if j == i - 1:
                              keys4 = lambda h: [("po", h, s_) for s_ in range(4)]
                              if d == 16:
                                  if r % 4 == 3:
                                      grp = r // 4
                                      for h in range(2):
                                          acc = accA if h == 0 else accB
                                          src = PB[5 + h].rearrange("p (a m) -> p a m", a=4)
                                          dstv = acc.rearrange("p (m x) -> p m x", x=16)[:, :, 4 * grp:4 * grp + 4].rearrange("p m a -> p a m")
                                          ev(P, first, dstv, src, h, keys4(h))
                              elif j % 4 == 3:
                                  jg = j // 4
                                  for h in range(2):
                                      acc = accA if h == 0 else accB
                                      src = PB[5 + h].rearrange("p (a m) -> p a m", a=4)
                                      if d == 4:
                                          dstv = acc.rearrange("p (m x) -> p m x", x=4)[:, :, r].rearrange("p (a m) -> p a m", a=4)
                                      else:
                                          dstv = acc[:, jg * 512:(jg + 1) * 512].rearrange("p (a m) -> p a m", a=4)
                                      ev(P, first, dstv, src, h, keys4(h))

                  NTK = len(tasks)
                  for n in range(NTK + 2):
                      if 1 <= n <= NTK:
                          ph2(n - 1, tasks[n - 1])
                      if n < NTK:
                          ph1(n, tasks[n])
                      if n >= 2:
                          ph3(n - 2, tasks[n - 2])
                  stage_end("Da")
                  P.op(A, lambda e: e.activation(out=Dt[0:64, :], in_=accA[64:128, :], func=AF.Copy), reads=["accA"], writes=["Dt"])
                  P.op(A, lambda e: e.activation(out=Dt[64:128, :], in_=accB[0:64, :], func=AF.Copy), reads=["accB"], writes=["Dt"])
                  P.op(A, lambda e: e.activation(out=Dt, in_=Dt, func=AF.Ln), reads=["Dt"], writes=["Dt"])
                  P.op(A, lambda e: e.activation(out=Dt, in_=Dt, func=AF.Exp, scale=-1.0), reads=["Dt"], writes=["Dt"])
                  P.op(V, lambda e: e.tensor_tensor(out=Dt, in0=Dt, in1=sgT, op=ALU.mult), reads=["Dt", "sgT"], writes=["Dt"])
                  P.op(V, (lambda hp: lambda e: e.tensor_tensor(out=catT[0:64, hp, :], in0=accA[0:64, :], in1=Dt[0:64, :], op=ALU.mult))(hp),
                       reads=["accA", "Dt"], writes=["catT"])
                  P.op(V, (lambda hp: lambda e: e.tensor_tensor(out=catT[64:128, hp, :], in0=accB[64:128, :], in1=Dt[64:128, :], op=ALU.mult))(hp),
                       reads=["accB", "Dt"], writes=["catT"])
                  stage_end("Df")
              P.barrier(dummy)
              stage_end("D")

              e_steps = make_E(job)
              if job + 1 < njob:
                  A_cur = make_A(job + 1)
                  a_steps = list(A_cur["steps"])
              else:
                  a_steps = []
              for st_ in e_steps:
                  st_()
                  if a_steps:
                      a_steps.pop(0)()
              for st_ in a_steps:
                  st_()
        except _Stop:
            pass

        P.wait_all(S)
        P.emit(st)
    return nc


def ev(P, first, dstv, src, h, keys):
    key = "accA" if h == 0 else "accB"
    if first:
        P.op("vector", lambda e: e.tensor_copy(out=dstv, in_=src), reads=keys, writes=[key])
    else:
        P.op("vector", lambda e: e.tensor_tensor(out=dstv, in0=dstv, in1=src, op=ALU.add), reads=keys + [key], writes=[key])


def host_consts():
    cm = np.zeros((128, 640), np.float32)
    cm[:, 0:128] = np.eye(128, dtype=np.float32)
    perm = np.zeros((128, 128), np.float32)
    for m in range(128):
        c = m % 64
        if c < 8:
            perm[m + 8, m] = 1.0
        elif c < 16:
            perm[m - 8, m] = 1.0
    cm[:, 128:256] = perm
    p = np.arange(128)[:, None]
    j = np.arange(256)[None, :]
    cm[:, 256:512] = ((j >= p) & (j <= p + 128)).astype(np.float32)
    cm[:, 512:640] = 1.0 / 512.0
    return cm


def rope_consts():
    invf = np.zeros(128, np.float32)
    sgn = np.zeros(128, np.float32)
    inv = (500000.0 ** (-np.arange(8, dtype=np.float32) * 2.0 / 16.0)).astype(np.float32)
    for m in range(128):
        c = m % 64
        if c < 16:
            invf[m] = inv[c % 8]
            sgn[m] = -1.0 if c < 8 else 1.0
    return invf, sgn


def job_meta(seq_len, q_start):
    wpos = np.arange(q_start - HALO, q_start - HALO + NE)
    valid = (wpos >= 0) & (wpos < seq_len)
    pos = np.where(valid, wpos, 0).astype(np.float32)
    kv = np.zeros((128, NKT), np.float32)
    pidx = np.arange(128)
    for n, (d, r, i, nt) in enumerate(KT_LIST):
        base = HALO // d
        m = base - 64 + 128 * i + pidx
        e = d * m + r
        kv[:, n] = valid[e].astype(np.float32)
    return pos, kv, valid


def make_inputs(x_prompt, x_sample, norm_pre, w_in, conv_w, conv_b, conv_ln_g, conv_ln_b, w_pw, b_pw, w_out, norm_post, njob=NJOB):
    f = np.float32
    w_in = np.asarray(w_in, f)[0]
    cm = host_consts()
    invf, sgn = rope_consts()
    prm = np.zeros((128, 160), f)
    prm[:, 0:8] = np.asarray(norm_pre, f)[0].reshape(8, 128).T
    prm[:, 8:12] = np.asarray(conv_b, f)[0].reshape(4, 128).T
    prm[:, 12:16] = np.asarray(conv_ln_g, f)[0].reshape(4, 128).T
    prm[:, 16:20] = np.asarray(conv_ln_b, f)[0].reshape(4, 128).T
    prm[:, 20:24] = np.asarray(b_pw, f)[0].reshape(4, 128).T
    cw = np.asarray(conv_w, f)[0]
    prm[:, 24:148] = cw.T.reshape(4, 128, 31).transpose(1, 0, 2).reshape(128, 124)
    prm[:, 148] = invf
    prm[:, 149] = sgn
    prm[:, 150] = (invf.astype(np.float64) / (2.0 * np.pi)).astype(f)
    prm[:, 151] = (sgn.astype(np.float64) * 2.0 * np.pi).astype(f)
    gpost = np.ascontiguousarray(np.broadcast_to(np.asarray(norm_post, f)[0][None, :], (128, D)))
    wk = w_in.reshape(8, 128, 3584).transpose(1, 0, 2)
    wcv = np.stack([np.concatenate([wk[:, :, 2048 + g * 128:2048 + (g + 1) * 128],
                                    wk[:, :, 2560 + g * 128:2560 + (g + 1) * 128],
                                    wk[:, :, 3072 + g * 128:3072 + (g + 1) * 128]], axis=2) for g in range(4)])
    whp = np.stack([np.concatenate([wk[:, :, 0 + h * 128:0 + (h + 1) * 128],
                                    wk[:, :, 512 + h * 128:512 + (h + 1) * 128],
                                    wk[:, :, 1024 + h * 128:1024 + (h + 1) * 128],
                                    wk[:, :, 1536 + h * 128:1536 + (h + 1) * 128]], axis=2) for h in range(4)])
    wpw = np.ascontiguousarray(np.asarray(w_pw, f)[0].reshape(4, 128, 512).transpose(1, 0, 2))
    wout = np.ascontiguousarray(np.asarray(w_out, f)[0].reshape(8, 128, 1024).transpose(1, 0, 2))
    shared = {"cmat": cm, "prm": prm, "gpost": gpost, "wcv": np.ascontiguousarray(wcv), "whp": np.ascontiguousarray(whp),
              "wpw": wpw, "wout": wout}
    xp = np.asarray(x_prompt, f)
    xs = np.asarray(x_sample, f)[0]
    S_P = xp.shape[1]
    S_S = xs.shape[0]
    in_maps = []
    for c in range(NCORE):
        xw = np.zeros((njob, NE, D), f)
        pos = np.zeros((njob, NE), f)
        kval = np.zeros((njob, 128, NKT), f)
        jobs = [("p", c, 0), ("p", c, NQ), ("s", 0, c * NQ)][:njob]
        for jn, (kind, b, q0) in enumerate(jobs):
            src = xp[b] if kind == "p" else xs
            slen = S_P if kind == "p" else S_S
            lo = q0 - HALO
            hi = lo + NE
            a = max(lo, 0)
            bnd = min(hi, slen)
            xw[jn, a - lo:bnd - lo] = src[a:bnd]
            p_, kv_, _ = job_meta(slen, q0)
            pos[jn] = p_
            kval[jn] = kv_
        m = {"xw": xw, "pos": pos, "kval": kval}
        m.update(shared)
        in_maps.append(m)
    return in_maps


_NC_CACHE = {}


def kernel(x_prompt, x_sample, norm_pre, w_in, conv_w, conv_b, conv_ln_g, conv_ln_b, w_pw, b_pw, w_out, norm_post):
    in_maps = make_inputs(x_prompt, x_sample, norm_pre, w_in, conv_w, conv_b, conv_ln_g, conv_ln_b, w_pw, b_pw, w_out, norm_post)
    if "nc" not in _NC_CACHE:
        _NC_CACHE["nc"] = build_program()
    nc = _NC_CACHE["nc"]
    res = run_bass_kernel_spmd(nc, in_maps, core_ids=list(range(NCORE)))
    B, SP = np.asarray(x_prompt).shape[0:2]
    SS = np.asarray(x_sample).shape[1]
    y_prompt = np.zeros((B, SP, D), np.float32)
    y_sample = np.zeros((1, SS, D), np.float32)
    for c in range(NCORE):
        y = res.results[c]["y"]
        y_prompt[c, 0:NQ] = y[0]
        y_prompt[c, NQ:2 * NQ] = y[1]
        y_sample[0, c * NQ:(c + 1) * NQ] = y[2]
    return (y_prompt, y_sample)
```

```python
import math
from contextlib import ExitStack

import numpy as np
import concourse.bass as bass
import concourse.mybir as mybir
from concourse.bass_utils import run_bass_kernel_spmd

F32 = mybir.dt.float32
BF16 = mybir.dt.bfloat16
I32 = mybir.dt.int32
AF = mybir.ActivationFunctionType
ALU = mybir.AluOpType

D = 1024
NQ = 2048
HALO = 1024
NE = NQ + 2 * HALO
NJOB = 3
NCORE = 8
PATTERNS = (16, 4, 1)
NEG = -30000.0
TWO_PI = 2.0 * math.pi

ENGS = ["sync", "scalar", "vector", "gpsimd", "tensor"]


def key_tile_list():
    out = []
    for d in PATTERNS:
        nt = NQ // (128 * d)
        for r in range(d):
            for i in range(nt + 1):
                out.append((d, r, i, nt))
    return out


def zero_window_tiles(job):
    if job == 0:
        return set(range(0, HALO // 128))
    if job == 1:
        return set(range((NE - HALO) // 128, NE // 128))
    return set()


KT_LIST = key_tile_list()
NKT = len(KT_LIST)


class Prog:
    def __init__(self, nc):
        self.nc = nc
        self.lists = {e: [] for e in ENGS}
        self.cnt = {e: 0 for e in ENGS}
        self.waited = {e: {} for e in ENGS}
        self.res = {}
        self.sems = {}
        self.dma_sems = {}
        self.dma_cnt = {}
        self.forced = {e: {} for e in ENGS}

    def _deps(self, eng, reads, writes, is_dma=False):
        need = dict(self.forced[eng])
        self.forced[eng] = {}
        skip_self = not is_dma

        def add(k, v):
            if need.get(k, 0) < v:
                need[k] = v

        for r in reads:
            st = self.res.get(r)
            if st:
                for k, v in st["w"].items():
                    add(k, v)
        for w in writes:
            st = self.res.get(w)
            if st:
                for k, v in st["w"].items():
                    if not (skip_self and k == eng):
                        add(k, v)
                for k, v in st["r"].items():
                    if not (skip_self and k == eng):
                        add(k, v)
        waits = []
        for k, v in need.items():
            if k == "tensor" and eng == "tensor":
                continue
            if self.waited[eng].get(k, 0) < v:
                self.waited[eng][k] = v
                waits.append((k, v))
        return waits

    def _mark(self, reads, writes, tok):
        for r in reads:
            st = self.res.setdefault(r, {"w": {}, "r": {}})
            st["r"][tok[0]] = tok[1]
        for w in writes:
            st = self.res.setdefault(w, {"w": {}, "r": {}})
            st["w"][tok[0]] = tok[1]

    def op(self, eng, fn, reads=(), writes=()):
        waits = self._deps(eng, reads, writes)
        self.cnt[eng] += 1
        tok = (eng, self.cnt[eng])
        self.lists[eng].append(("op", waits, fn, None))
        self._mark(reads, writes, tok)
        return tok

    def dma(self, queue, out, in_, semkey, reads=(), writes=(), **kw):
        waits = self._deps(queue, reads, writes, is_dma=True)
        self.dma_cnt[semkey] = self.dma_cnt.get(semkey, 0) + 16
        tok = (("dma", semkey), self.dma_cnt[semkey])
        self.lists[queue].append(("dma", waits, (out, in_, kw), semkey))
        self._mark(reads, writes, tok)
        return tok

    def barrier(self, dummy):
        waits = [(e, self.cnt[e]) for e in ENGS if e != "gpsimd" and self.cnt[e] > 0]
        waits += [(("dma", k), v) for k, v in self.dma_cnt.items()]
        waits.append(("gpsimd", self.cnt["gpsimd"]))
        w2 = []
        for k, v in waits:
            if v > 0 and self.waited["gpsimd"].get(k, 0) < v:
                self.waited["gpsimd"][k] = v
                w2.append((k, v))
        self.cnt["gpsimd"] += 1
        tok = ("gpsimd", self.cnt["gpsimd"])
        self.lists["gpsimd"].append(("op", w2, lambda e: e.memset(dummy, 0.0), None))
        for e in ENGS:
            self.forced[e]["gpsimd"] = tok[1]

    def wait_all(self, eng):
        waits = [(("dma", k), v) for k, v in self.dma_cnt.items()]
        waits += [(e, self.cnt[e]) for e in ENGS if self.cnt[e] > 0 and e != eng]
        self.lists[eng].append(("wait", waits, None, None))

    def emit(self, stack):
        nc = self.nc
        for e in ENGS:
            self.sems[e] = stack.enter_context(nc.semaphore("s_" + e))
        for k in self.dma_cnt:
            self.dma_sems[k] = stack.enter_context(nc.semaphore("d_" + str(k)))
        block = stack.enter_context(nc.Block())

        def semof(k):
            if isinstance(k, tuple):
                return self.dma_sems[k[1]]
            return self.sems[k]

        def run(ename):
            def body(engine):
                for kind, waits, payload, semkey in self.lists[ename]:
                    for k, v in waits:
                        engine.wait_ge(semof(k), v)
                    if kind == "op":
                        payload(engine).then_inc(self.sems[ename], 1)
                    elif kind == "dma":
                        out, in_, kw = payload
                        engine.dma_start(out=out, in_=in_, **kw).then_inc(self.dma_sems[semkey], 16)
            return body

        block.sync(run("sync"))
        block.scalar(run("scalar"))
        block.vector(run("vector"))
        block.gpsimd(run("gpsimd"))
        block.tensor(run("tensor"))


_LASTP = [None]


class _Stop(Exception):
    pass


def build_program(njob=NJOB, dbg=None, upto=None, ktlimit=None, sub=None):
    nc = bass.Bass("TRN2", target_bir_lowering=False)
    dt = nc.dram_tensor
    xw = dt("xw", [njob, NE, D], F32, kind="ExternalInput").ap()
    posd = dt("pos", [njob, NE], F32, kind="ExternalInput").ap()
    kvald = dt("kval", [njob, 128, NKT], F32, kind="ExternalInput").ap()
    cmatd = dt("cmat", [128, 640], F32, kind="ExternalInput").ap()
    prmd = dt("prm", [128, 160], F32, kind="ExternalInput").ap()
    gpostd = dt("gpost", [128, D], F32, kind="ExternalInput").ap()
    wcvd = dt("wcv", [4, 128, 8, 384], F32, kind="ExternalInput").ap()
    whpd = dt("whp", [4, 128, 8, 512], F32, kind="ExternalInput").ap()
    wpwd = dt("wpw", [128, 4, 512], F32, kind="ExternalInput").ap()
    woutd = dt("wout", [128, 8, 1024], F32, kind="ExternalInput").ap()
    yd = dt("y", [njob, NQ, D], F32, kind="ExternalOutput").ap()
    dbg_out = {}
    if dbg:
        for name, shape in dbg.items():
            dbg_out[name] = dt("dbg_" + name, list(shape), F32, kind="ExternalOutput").ap()

    st = ExitStack()
    with st:
        sb = lambda name, shape, dtype: st.enter_context(nc.sbuf_tensor("s_" + name, shape, dtype))
        hT = sb("hT", [128, 8, NE], BF16)
        catT2 = sb("catT", [128, 8 * NQ], BF16)
        catT = catT2[:, :].rearrange("p (k n) -> p k n", k=8)
        XR = sb("XR", [128, 12288], F32)
        ropeC = sb("ropeC", [128, NE], BF16)
        ropeS = sb("ropeS", [128, NE], BF16)
        wst = [sb("wst%d" % i, [128, 2, 512], F32) for i in range(2)]
        wblk = [sb("wblk%d" % i, [128, 8, 512], BF16) for i in range(2)]
        wpw = sb("wpw", [128, 4, 512], BF16)
        cb = sb("cb", [128, 640], BF16)
        prm = sb("prm", [128, 160], F32)
        vbt = sb("vbt", [128, NKT], F32)
        small = sb("small", [128, 16], F32)
        small2 = sb("small2", [128, 32], F32)
        SCR = sb("SCR", [128, 4096], F32)
        PSALL = st.enter_context(nc.psum_tensor("psall", [128, 4096], F32))
        PB = [PSALL[:, 512 * i:512 * (i + 1)] for i in range(8)]
        PT16 = PSALL[:, 3584:4096].bitcast(BF16)

        P = Prog(nc)
        _LASTP[0] = P
        V, A, G, T, S = "vector", "scalar", "gpsimd", "tensor", "sync"
        cf = XR[:, 8192:8832]

        ident_b = cb[:, 0:128]
        perm_b = cb[:, 128:256]
        band_b = cb[:, 256:512]
        onesm_b = cb[:, 512:640]
        gpre = prm[:, 0:8]
        invf = prm[:, 148:149]
        sgn = prm[:, 149:150]
        eps_rms = small[:, 0:1]
        eps_ln = small[:, 1:2]
        dummy = small[:, 8:9]

        P.dma(S, cf[:], cmatd[:, :], "c0", writes=["cf"])
        P.dma(S, prm[:], prmd[:, :], "c1", writes=["prm"])
        P.op(V, lambda e: e.tensor_copy(out=cb[:], in_=cf[:]), reads=["cf"], writes=["cb"])
        P.op(G, lambda e: e.memset(small[:, 0:1], 1e-6), writes=["small"])
        P.op(G, lambda e: e.memset(small[:, 1:2], 1e-5), writes=["small"])
        for hlf in range(2):
            P.dma(S, wst[hlf][:], wpwd[:, 2 * hlf:2 * hlf + 2, :], "wst%d" % hlf, writes=[("wst", hlf)])
            P.op(G, (lambda h: lambda e: e.tensor_copy(out=wpw[:, 2 * h:2 * h + 2, :], in_=wst[h][:]))(hlf),
                 reads=[("wst", hlf)], writes=["wpw"])

        wst_ctr = [0]

        def load_wblock(src, slot, W, gain=True):
            for q4 in range(4):
                s = wst_ctr[0] % 2
                wst_ctr[0] += 1
                P.dma(S, wst[s][:, :, 0:W], src[:, 2 * q4:2 * q4 + 2, :], "wst%d" % s, writes=[("wst", s)])
                if gain:
                    P.op(V, (lambda s, q4: lambda e: e.tensor_tensor(
                        out=wblk[slot][:, 2 * q4:2 * q4 + 2, 0:W], in0=wst[s][:, :, 0:W],
                        in1=gpre[:, 2 * q4:2 * q4 + 2].unsqueeze(2).to_broadcast([128, 2, W]), op=ALU.mult))(s, q4),
                        reads=[("wst", s), "prm"], writes=[("wblk", slot)])
                else:
                    P.op(V, (lambda s, q4: lambda e: e.tensor_copy(
                        out=wblk[slot][:, 2 * q4:2 * q4 + 2, 0:W], in_=wst[s][:, :, 0:W]))(s, q4),
                        reads=[("wst", s)], writes=[("wblk", slot)])

        def proj(ps, slot, c0, t0, n, first_reads=()):
            hkeys = [("hT", tt) for tt in range(t0 // 128, (t0 + n - 1) // 128 + 1)]
            for kc in range(8):
                P.op(T, (lambda kc: lambda e: e.matmul(ps, lhsT=wblk[slot][:, kc, c0:c0 + 128], rhs=hT[:, kc, t0:t0 + n],
                                                      start=(kc == 0), stop=(kc == 7)))(kc),
                     reads=[("wblk", slot)] + hkeys, writes=[ps_key(ps)])

        psk = {}

        def ps_key(ap):
            return psk[id(ap)]

        def pv(bank, a, b, key=None):
            ap = PB[bank][:, a:b]
            psk[id(ap)] = key if key is not None else ("pb", bank)
            return ap

        def dump(name, ap_src, rows, reads):
            if dbg and name in dbg_out:
                P.dma(S, dbg_out[name], ap_src, "dbg", reads=reads)

        def stage_end(name):
            if upto == name:
                raise _Stop()

        PIS = 3.1415925
        PT16s = [PB[7 - 0] if False else PSALL[:, 3584:4096].bitcast(BF16), PSALL[:, 2048:2560].bitcast(BF16)]
        PT16k = [("pb", 7), ("pb", 4)]
        PT16b6 = PSALL[:, 3072:3584].bitcast(BF16)
        def make_A(job):
            xt = [SCR[:, 0:1024], SCR[:, 1024:2048], SCR[:, 2048:3072]]
            hb = [SCR[:, 3072:3584].bitcast(BF16), SCR[:, 3584:4096].bitcast(BF16)]
            ptA = [PT16s[0], PT16b6]
            ptAk = [("pb", 7), ("pb", 6)]
            NT = NE // 128

            acnt = {}

            def a1_dma(t):
                acnt[t] = len(acnt)
                x3 = acnt[t] % 3
                P.dma(S, xt[x3], xw[job, t * 128:(t + 1) * 128, :], "x%d" % x3, writes=[("xt", x3)])

            def a1_norm(t):
                s = acnt[t] % 2
                x3 = acnt[t] % 3
                ss = small[:, 2 + s:3 + s]
                P.op(A, lambda e: e.activation(out=hb[s], in_=xt[x3], func=AF.Square, accum_out=ss), reads=[("xt", x3)], writes=[("hb", s), ("ss", s)])
                P.op(A, lambda e: e.activation(out=ss, in_=ss, func=AF.Sqrt, bias=eps_rms, scale=1.0 / D), reads=[("ss", s), "small"], writes=[("ss", s)])
                P.op(V, lambda e: e.reciprocal(out=ss, in_=ss), reads=[("ss", s)], writes=[("ss", s)])
                P.op(V, lambda e: e.tensor_scalar(out=hb[s], in0=xt[x3], scalar1=ss, scalar2=None, op0=ALU.mult),
                     reads=[("xt", x3), ("ss", s)], writes=[("hb", s)])

            def a1(t):
                a1_dma(t)
                a1_norm(t)


            def a2(t):
                s = acnt[t] % 2
                for kc in range(8):
                    P.op(T, (lambda kc: lambda e: e.transpose(ptA[s][:, kc * 128:(kc + 1) * 128], hb[s][:, kc * 128:(kc + 1) * 128], ident_b))(kc),
                         reads=[("hb", s), "cb"], writes=[ptAk[s]])
                P.op(V, lambda e: e.tensor_copy(out=hT[:, :, t * 128:(t + 1) * 128], in_=ptA[s].rearrange("p (k n) -> p k n", n=128)),
                     reads=[ptAk[s]], writes=[("hT", t)])

            zt = zero_window_tiles(job)
            first_tiles = [t for t in range(7, 25) if t not in zt]
            deferred = [t for t in range(NT) if t not in first_tiles and t not in zt]
            for t in sorted(zt):
                if 7 <= t < 25:
                    P.op(G, (lambda t: lambda e: e.memset(hT[:, :, t * 128:(t + 1) * 128], 0.0))(t), writes=[("hT", t)])
            RPN = 512
            rbase = 9216
            rsets = [(XR[:, rbase + 3 * RPN * i:rbase + 3 * RPN * i + RPN], XR[:, rbase + 3 * RPN * i + RPN:rbase + 3 * RPN * i + 2 * RPN],
                      XR[:, rbase + 3 * RPN * i + 2 * RPN:rbase + 3 * RPN * i + 3 * RPN]) for i in range(2)]
            halfpi = small[:, 9:10]
            P.op(G, lambda e: e.memset(halfpi, 0.5 * math.pi), writes=["small"])

            def rope_a(k):
                rp_pos, rp_s, rp_a = rsets[k % 2]
                kp, ks, ka = ("posf", k % 2), ("s1", k % 2), ("ra", k % 2)
                rp_si = rp_s.bitcast(I32)
                c0 = RPN * k
                P.dma(S, rp_pos, posd[job, c0:c0 + RPN].partition_broadcast(128), "r%d" % (k % 2), writes=[kp])
                P.op(V, lambda e: e.tensor_scalar(out=rp_pos, in0=rp_pos, scalar1=prm[:, 150:151], scalar2=None, op0=ALU.mult), reads=[kp, "prm"], writes=[kp])
                P.op(V, lambda e: e.tensor_copy(out=rp_si, in_=rp_pos), reads=[kp], writes=[ks, ka])
                P.op(V, lambda e: e.tensor_copy(out=rp_s, in_=rp_si), reads=[ks], writes=[ks])
                P.op(V, lambda e: e.tensor_tensor(out=rp_s, in0=rp_pos, in1=rp_s, op=ALU.subtract), reads=[ks, kp], writes=[ks])
                P.op(V, lambda e: e.tensor_scalar(out=rp_s, in0=rp_s, scalar1=-0.4999999, scalar2=0.4999999, op0=ALU.max, op1=ALU.min), reads=[ks], writes=[ks])

            def rope_b(k):
                rp_pos, rp_s, rp_a = rsets[k % 2]
                ks, ka = ("s1", k % 2), ("ra", k % 2)
                c0 = RPN * k
                P.op(A, lambda e: e.activation(out=rp_a, in_=rp_s, func=AF.Sin, scale=math.pi), reads=[ks], writes=[ka])
                P.op(A, lambda e: e.activation(out=ropeS[:, c0:c0 + RPN], in_=rp_s, func=AF.Sin, scale=prm[:, 151:152]), reads=[ks, "prm"], writes=["ropeS"])

            def rope_c(k):
                rp_pos, rp_s, rp_a = rsets[k % 2]
                ks, ka = ("s1", k % 2), ("ra", k % 2)
                c0 = RPN * k
                P.op(V, lambda e: e.tensor_tensor(out=rp_a, in0=rp_a, in1=rp_a, op=ALU.mult), reads=[ka], writes=[ka])
                P.op(V, lambda e: e.tensor_scalar(out=ropeC[:, c0:c0 + RPN], in0=rp_a, scalar1=-2.0, scalar2=1.0, op0=ALU.mult, op1=ALU.add),
                     reads=[ka], writes=["ropeC"])

            NRP = NE // RPN
            rstep = [0]

            def rope_step():
                r = rstep[0]
                rstep[0] += 1
                if 0 <= r - 2 < NRP:
                    rope_c(r - 2)
                if 0 <= r - 1 < NRP:
                    rope_b(r - 1)
                if r < NRP:
                    rope_a(r)

            steps = []

            def mk(n_, t):
                def f():
                    if t is not None:
                        a1(t)
                    if n_ >= 1:
                        a2(first_tiles[n_ - 1])
                    pass
                return f

            for n_, t in enumerate(first_tiles + [None]):
                steps.append(mk(n_, t))
            return dict(steps=steps, a1_dma=a1_dma, a1_norm=a1_norm, a2=a2, deferred=deferred)

        def make_E(job):
            gpb = XR[:, 0:1024]
            xt2s = [XR[:, 1024 * (1 + i):1024 * (2 + i)] for i in range(4)]
            ots = [XR[:, 5120:6144], XR[:, 6144:7168]]
            P.dma(S, gpb, gpostd[:, :], "c3", writes=["gpb"])
            junk2s = [XR[:, 7168:7680].bitcast(BF16), XR[:, 7680:8192].bitcast(BF16)]
            pys = {}
            ssE0 = small2[:, 0:16]
            ssE1 = small2[:, 16:32]

            def e1(t):
                s = t % 2
                x4 = t % 4
                P.dma(S, xt2s[x4], xw[job, HALO + t * 128:HALO + (t + 1) * 128, :], "xe%d" % x4, writes=[("xt2", x4)])
                b4 = 2 * (t % 3)
                py = [pv(b4, 0, 512), pv(b4 + 1, 0, 512)]
                pys[t] = py
                for nh in range(2):
                    for kc in range(8):
                        P.op(T, (lambda nh, kc: lambda e: e.matmul(py[nh], lhsT=catT[:, kc, t * 128:(t + 1) * 128], rhs=wblk[nh][:, kc, :],
                                                                   start=(kc == 0), stop=(kc == 7)))(nh, kc),
                             reads=["catT", ("wblk", nh)], writes=[("pb", b4 + nh)])
                P.op(A, lambda e: e.activation(out=junk2s[s][:, 0:1024], in_=PSALL[:, 512 * b4:512 * (b4 + 2)], func=AF.Square,
                                               accum_out=ssE0[:, t:t + 1]),
                     reads=[("pb", b4), ("pb", b4 + 1)], writes=[("junk2", s), ("ssE", t // 2)])

            def e2(t):
                s = t % 2
                py = pys[t]
                ot = ots[s]
                xt2 = xt2s[t % 4]
                k = ("ssE", t // 2)
                a0 = ssE0[:, t:t + 1]
                P.op(A, lambda e: e.activation(out=a0, in_=a0, func=AF.Sqrt, bias=eps_rms, scale=1.0 / D), reads=[k, "small"], writes=[k])
                P.op(V, lambda e: e.reciprocal(out=a0, in_=a0), reads=[k], writes=[k])
                for nh in range(2):
                    P.op(V, (lambda nh: lambda e: e.scalar_tensor_tensor(out=ot[:, nh * 512:(nh + 1) * 512], in0=py[nh], scalar=a0,
                                                                         in1=gpb[:, nh * 512:(nh + 1) * 512], op0=ALU.mult, op1=ALU.mult))(nh),
                         reads=[("pb", 2 * (t % 3) + nh), k, "gpb"], writes=[("ot", s)])
                P.op(V, lambda e: e.tensor_tensor(out=ot, in0=ot, in1=xt2, op=ALU.add), reads=[("ot", s), ("xt2", t % 4)], writes=[("ot", s)])
                P.dma(G, yd[job, t * 128:(t + 1) * 128, :], ot, "st%d" % s, reads=[("ot", s)])


            NTE = NQ // 128
            steps = []

            def mk(t):
                def f():
                    if t < NTE:
                        e1(t)
                    if t >= 1:
                        e2(t - 1)
                return f

            for t in range(NTE + 1):
                steps.append(mk(t))
            return steps

        try:
          A_cur = make_A(0)
          for st_ in A_cur["steps"]:
              st_()
          for job in range(njob):
              a1_dma, a1_norm, a2, deferred = A_cur["a1_dma"], A_cur["a1_norm"], A_cur["a2"], A_cur["deferred"]
              load_wblock(wcvd[0], 0, 384)
              P.dma(S, vbt[:], kvald[job], "c2", writes=["vbt"])
              P.op(V, lambda e: e.tensor_scalar(out=vbt[:], in0=vbt[:], scalar1=-1.0, scalar2=-NEG, op0=ALU.add, op1=ALU.mult),
                   reads=["vbt"], writes=["vbt"])


              P.barrier(dummy)
              stage_end("A")

              uT = XR[:, 0:4160].bitcast(BF16).rearrange("p (g n) -> p g n", g=4)
              diag = XR[:, 4160:4160 + 7936].bitcast(BF16).rearrange("p (g j n) -> p g j n", g=4, j=31)
              cscr0 = catT2[:, 0:4 * NQ].bitcast(F32)
              sigts = [cscr0[:, 0:512], cscr0[:, 512:1024]]
              sigts2 = [cscr0[:, 1536:2048], cscr0[:, 2048:2560]]
              dq = list(deferred)
              dstep = [0]
              u0 = HALO - 16
              pieces = [(u0 + 512 * c, 512) for c in range(4)] + [(u0 + 2048, 32)]
              pn = 0
              for g in range(4):
                  slot = g % 2
                  if g < 3:
                      load_wblock(wcvd[g + 1], (g + 1) % 2, 384)
                  else:
                      load_wblock(whpd[0], 0, 512)
                  for pidx, (t0, n) in enumerate(pieces):
                      if pidx == 2:
                          P.op(V, (lambda gg: lambda e: e.tensor_tensor(
                              out=diag[:, gg, :, :], in0=ident_b.unsqueeze(1).to_broadcast([128, 31, 128]),
                              in1=prm[:, 24 + gg * 31:24 + gg * 31 + 31].unsqueeze(2).to_broadcast([128, 31, 128]), op=ALU.mult))(g),
                              reads=["cb", "prm"], writes=["diag"])
                      bs = 3 * (pn % 2)
                      sg_ = sigts[pn % 2]
                      sk = ("sigt", pn % 2)
                      pn += 1
                      pa = pv(bs + 0, 0, n)
                      pb_ = pv(bs + 1, 0, n)
                      proj(pa, slot, 0, t0, n)
                      proj(pb_, slot, 128, t0, n)
                      P.op(A, (lambda pb_, n, sg_: lambda e: e.activation(out=sg_[:, 0:n], in_=pb_, func=AF.Sigmoid))(pb_, n, sg_),
                           reads=[ps_key(pb_)], writes=[sk])
                      P.op(V, (lambda pa, n, g, t0, sg_: lambda e: e.tensor_tensor(out=uT[:, g, t0 - u0:t0 - u0 + n], in0=pa, in1=sg_[:, 0:n],
                                                                                    op=ALU.mult))(pa, n, g, t0, sg_),
                           reads=[ps_key(pa), sk], writes=["uT"])
                      if n == 512:
                          pg = pv(bs + 2, 0, 512)
                          proj(pg, slot, 256, t0 + 16, 512)
                          q0 = t0 + 16 - HALO
                          sg2 = sigts2[pn % 2]
                          sk2 = ("sigt2", pn % 2)
                          P.op(A, (lambda pg, sg2: lambda e: e.activation(out=sg2, in_=pg, func=AF.Sigmoid))(pg, sg2),
                               reads=[ps_key(pg)], writes=[sk2])
                          P.op(V, (lambda pg, g, q0, sg2: lambda e: e.tensor_tensor(out=catT[:, 4 + g, q0:q0 + 512], in0=pg, in1=sg2, op=ALU.mult))(pg, g, q0, sg2),
                               reads=[ps_key(pg), sk2], writes=["catT"])
                      nsteps = 1
                      for _ in range(nsteps):
                          kq = dstep[0]
                          dstep[0] += 1
                          if 0 <= kq - 2 < len(dq):
                              a2(dq[kq - 2])
                          if 0 <= kq - 1 < len(dq):
                              a1_norm(dq[kq - 1])
                          if kq < len(dq):
                              a1_dma(dq[kq])
              while dstep[0] - 2 < len(dq):
                  kq = dstep[0]
                  dstep[0] += 1
                  if 0 <= kq - 2 < len(dq):
                      a2(dq[kq - 2])
                  if 0 <= kq - 1 < len(dq):
                      a1_norm(dq[kq - 1])
                  if kq < len(dq):
                      a1_dma(dq[kq])
              P.barrier(dummy)
              stage_end("C1")
              CW = 512
              NCH = NQ // CW
              cscr = catT2[:, 0:4 * NQ].bitcast(F32)
              co = cscr[:, 0:2048].rearrange("p (g n) -> p g n", g=4)
              cob = cscr[:, 2048:3072].bitcast(BF16).rearrange("p (g n) -> p g n", g=4)
              sqb = cscr[:, 3072:4096].bitcast(BF16).rearrange("p (g n) -> p g n", g=4)
              slb = SCR[:, 0:1024].bitcast(BF16).rearrange("p (g n) -> p g n", g=4)
              m2 = SCR[:, 1024:1536]
              rstd = SCR[:, 1536:2048]
              nmr = SCR[:, 2048:2560]
              pcs = [pv(g, 0, CW) for g in range(4)]
              pm = pv(4, 0, CW)
              pq = pv(5, 0, CW)
              pws = [pv(6, 0, CW), pv(7, 0, CW)]

              def c2_conv(c, groups=(0, 1, 2, 3)):
                  q0 = c * CW
                  for g in groups:
                      for j in range(31):
                          off = q0 + 1 + j
                          P.op(T, (lambda g, j, off: lambda e: e.matmul(pcs[g], lhsT=diag[:, g, j, :], rhs=uT[:, g, off:off + CW],
                                                                        start=(j == 0), stop=(j == 30)))(g, j, off),
                               reads=["diag", "uT"], writes=[("pb", g)])

              def c2_evac(c):
                  for g in range(4):
                      bia = prm[:, 8 + g:9 + g]
                      P.op(A, (lambda g, bia: lambda e: e.activation(out=co[:, g, :], in_=pcs[g], func=AF.Identity, bias=bia, scale=1.0))(g, bia),
                           reads=[("pb", g), "prm"], writes=[("co", g)])
                      P.op(A, (lambda g, bia: lambda e: e.activation(out=cob[:, g, :], in_=pcs[g], func=AF.Identity, bias=bia, scale=1.0))(g, bia),
                           reads=[("pb", g), "prm"], writes=[("cob", g)])
                      P.op(A, (lambda g, bia: lambda e: e.activation(out=sqb[:, g, :], in_=pcs[g], func=AF.Square, bias=bia, scale=1.0))(g, bia),
                           reads=[("pb", g), "prm"], writes=[("sqb", g)])

              def c2_stats(c):
                  for g in range(4):
                      P.op(T, (lambda g: lambda e: e.matmul(pm, lhsT=onesm_b, rhs=cob[:, g, :], start=(g == 0), stop=(g == 3)))(g),
                           reads=["cb", ("cob", g)], writes=[("pb", 4)])
                  for g in range(4):
                      P.op(T, (lambda g: lambda e: e.matmul(pq, lhsT=onesm_b, rhs=sqb[:, g, :], start=(g == 0), stop=(g == 3)))(g),
                           reads=["cb", ("sqb", g)], writes=[("pb", 5)])

              def c2_rest(c):
                  q0 = c * CW
                  P.op(A, lambda e: e.activation(out=m2, in_=pm, func=AF.Square), reads=[("pb", 4)], writes=["m2"])
                  P.op(V, lambda e: e.tensor_tensor(out=m2, in0=pq, in1=m2, op=ALU.subtract), reads=[("pb", 5), "m2"], writes=["m2"])
                  P.op(A, lambda e: e.activation(out=rstd, in_=m2, func=AF.Sqrt, bias=eps_ln, scale=1.0), reads=["m2", "small"], writes=["rstd"])
                  P.op(V, lambda e: e.reciprocal(out=rstd, in_=rstd), reads=["rstd"], writes=["rstd"])
                  P.op(V, lambda e: e.scalar_tensor_tensor(out=nmr, in0=pm, scalar=-1.0, in1=rstd, op0=ALU.mult, op1=ALU.mult),
                       reads=[("pb", 4), "rstd"], writes=["nmr"])
                  for g in range(4):
                      P.op(V, (lambda g: lambda e: e.tensor_tensor(out=co[:, g, :], in0=co[:, g, :], in1=rstd, op=ALU.mult))(g),
                           reads=[("co", g), "rstd"], writes=[("co", g)])
                      P.op(V, (lambda g: lambda e: e.tensor_tensor(out=co[:, g, :], in0=co[:, g, :], in1=nmr, op=ALU.add))(g),
                           reads=[("co", g), "nmr"], writes=[("co", g)])
                      P.op(A, (lambda g: lambda e: e.activation(out=slb[:, g, :], in_=co[:, g, :], func=AF.Silu,
                                                                bias=prm[:, 16 + g:17 + g], scale=prm[:, 12 + g:13 + g]))(g),
                           reads=[("co", g), "prm"], writes=[("slb", g)])
                  for go in range(4):
                      pw_ = pws[go % 2]
                      for gi in range(4):
                          P.op(T, (lambda pw_, go, gi: lambda e: e.matmul(pw_, lhsT=wpw[:, gi, go * 128:(go + 1) * 128], rhs=slb[:, gi, :],
                                                                          start=(gi == 0), stop=(gi == 3)))(pw_, go, gi),
                               reads=["wpw", ("slb", gi)], writes=[("pb", 6 + go % 2)])
                      P.op(V, (lambda pw_, go, q0: lambda e: e.scalar_tensor_tensor(
                          out=catT[:, 4 + go, q0:q0 + CW], in0=pw_, scalar=prm[:, 20 + go:21 + go], in1=catT[:, 4 + go, q0:q0 + CW],
                          op0=ALU.add, op1=ALU.mult))(pw_, go, q0),
                          reads=[("pb", 6 + go % 2), "prm", "catT"], writes=["catT"])

              RP2 = 512
              q_pos, q_s, q_a = SCR[:, 2560:3072], SCR[:, 3072:3584], SCR[:, 3584:4096]
              q_si = q_s.bitcast(I32)

              def rp_dma(k):
                  P.dma(S, q_pos, posd[job, RP2 * k:RP2 * (k + 1)].partition_broadcast(128), "r0", writes=["qpos"])

              def rp_a(k):
                  P.op(V, lambda e: e.tensor_scalar(out=q_pos, in0=q_pos, scalar1=prm[:, 150:151], scalar2=None, op0=ALU.mult), reads=["qpos", "prm"], writes=["qpos"])
                  P.op(V, lambda e: e.tensor_copy(out=q_si, in_=q_pos), reads=["qpos"], writes=["qs", "qa"])
                  P.op(V, lambda e: e.tensor_copy(out=q_s, in_=q_si), reads=["qs"], writes=["qs"])
                  P.op(V, lambda e: e.tensor_tensor(out=q_s, in0=q_pos, in1=q_s, op=ALU.subtract), reads=["qs", "qpos"], writes=["qs"])
                  P.op(V, lambda e: e.tensor_scalar(out=q_s, in0=q_s, scalar1=-0.4999999, scalar2=0.4999999, op0=ALU.max, op1=ALU.min), reads=["qs"], writes=["qs"])
                  if k + 1 < NE // RP2:
                      rp_dma(k + 1)

              def rp_b(k):
                  c0 = RP2 * k
                  P.op(A, lambda e: e.activation(out=q_a, in_=q_s, func=AF.Sin, scale=math.pi), reads=["qs"], writes=["qa"])
                  P.op(A, lambda e: e.activation(out=ropeS[:, c0:c0 + RP2], in_=q_s, func=AF.Sin, scale=prm[:, 151:152]), reads=["qs", "prm"], writes=["ropeS"])

              def rp_c(k):
                  c0 = RP2 * k
                  P.op(V, lambda e: e.tensor_tensor(out=q_a, in0=q_a, in1=q_a, op=ALU.mult), reads=["qa"], writes=["qa"])
                  P.op(V, lambda e: e.tensor_scalar(out=ropeC[:, c0:c0 + RP2], in0=q_a, scalar1=-2.0, scalar2=1.0, op0=ALU.mult, op1=ALU.add),
                       reads=["qa"], writes=["ropeC"])

              rp_dma(0)
              c2_conv(0)
              for c in range(NCH):
                  c2_evac(c)
                  rp_a(2 * c)
                  if c + 1 < NCH:
                      c2_conv(c + 1, groups=(0,))
                  rp_b(2 * c)
                  c2_stats(c)
                  rp_c(2 * c)
                  rp_a(2 * c + 1)
                  if c + 1 < NCH:
                      c2_conv(c + 1, groups=(1, 2, 3))
                  rp_b(2 * c + 1)
                  c2_rest(c)
                  rp_c(2 * c + 1)
              c2keys = ["uT", "diag", "m2", "rstd", "nmr", "qpos", "qs", "qa"] + [(nm, g) for nm in ("co", "cob", "sqb", "slb") for g in range(4)]
              dkeys = ["kq0", "kq1", "vT", "sgT", "accA", "accB", "Dt", "catT", ("PT", 0), ("PT", 1), ("Vt", 0), ("Vt", 1), ("Vt", 2),
                       ("kraw", 0), ("kraw", 1), ("t1", 0), ("t1", 1)]
              P.op(G, lambda e: e.memset(dummy, 0.0), reads=[], writes=c2keys + dkeys)
              stage_end("C2")

              XB = XR[:, 0:6144].bitcast(BF16)
              kT = XB[:, 0:4096]
              vT = XB[:, 4096:8192]
              qT = XB[:, 8192:10240]
              sgT = XB[:, 10240:12288]
              accA = XR[:, 6144:8192]
              accB = XR[:, 8192:10240]
              Dt = XR[:, 10240:12288]
              kraws = [SCR[:, 2560:2816].bitcast(BF16), SCR[:, 2816:3072].bitcast(BF16)]
              t1s = [SCR[:, 3072:3584], SCR[:, 3584:4096]]
              PTt = [SCR[:, 0:256].bitcast(BF16), SCR[:, 256:512].bitcast(BF16)]
              Vt = [SCR[:, 512 + 96 * i:512 + 96 * (i + 1)].bitcast(BF16) for i in range(3)]
              for i in range(3):
                  P.op(G, (lambda i: lambda e: e.memset(Vt[i][:, 64:128], 1.0))(i), writes=[("Vt", i)])
              accbanks = [0, 1, 4]
              cn = [0]
              for hp in range(4):
                  slot = hp % 2
                  if hp < 3:
                      load_wblock(whpd[hp + 1], (hp + 1) % 2, 512)
                  else:
                      load_wblock(woutd[:, :, 0:512], 0, 512, gain=False)
                  pending = []

                  def rope_unit(u):
                      pp, kraw, t1, kk, tk, t0, dst, ch, which = u
                      P.op(T, lambda e: e.matmul(pp, lhsT=perm_b, rhs=kraw, start=True, stop=True), reads=["cb", kk], writes=[ps_key(pp)])
                      P.op(V, lambda e: e.tensor_tensor(out=t1, in0=pp, in1=ropeS[:, t0:t0 + 512], op=ALU.mult), reads=[ps_key(pp), "ropeS"], writes=[tk])
                      P.op(V, lambda e: e.tensor_tensor(out=kraw, in0=kraw, in1=ropeC[:, t0:t0 + 512], op=ALU.mult), reads=[kk, "ropeC"], writes=[kk])
                      P.op(V, lambda e: e.tensor_tensor(out=dst[:, ch * 512:(ch + 1) * 512], in0=kraw, in1=t1, op=ALU.add),
                           reads=[kk, tk], writes=["kq%d" % which])

                  zc = set(t // 4 for t in zero_window_tiles(job))
                  for ch in sorted(zc):
                      P.op(G, (lambda ch: lambda e: e.memset(kT[:, ch * 512:(ch + 1) * 512], 0.0))(ch), writes=["kq0"])
                      P.op(G, (lambda ch: lambda e: e.memset(vT[:, ch * 512:(ch + 1) * 512], 0.0))(ch), writes=["vT"])
                  for which, (c0, tbase, nch, dst) in enumerate(((128, 0, NE // 512, kT), (0, HALO, NQ // 512, qT))):
                      for ch in range(nch):
                          if which == 0 and ch in zc:
                              continue
                          t0 = tbase + ch * 512
                          n_ = cn[0]
                          cn[0] += 1
                          kraw = kraws[n_ % 2]
                          t1 = t1s[n_ % 2]
                          kk = ("kraw", n_ % 2)
                          tk = ("t1", n_ % 2)
                          pk = pv(accbanks[n_ % 3], 0, 512)
                          proj(pk, slot, c0, t0, 512)
                          P.op(A, (lambda pk, kraw: lambda e: e.activation(out=kraw, in_=pk, func=AF.Copy))(pk, kraw), reads=[ps_key(pk)], writes=[kk])
                          pp = pv(2 + n_ % 2, 0, 512)
                          if pending:
                              rope_unit(pending.pop(0))
                          pending.append((pp, kraw, t1, kk, tk, t0, dst, ch, which))
                  for ch in range(NE // 512):
                      if ch in zc:
                          continue
                      n_ = cn[0]
                      cn[0] += 1
                      pv_ = pv(accbanks[n_ % 3], 0, 512)
                      proj(pv_, slot, 256, ch * 512, 512)
                      if pending:
                          rope_unit(pending.pop(0))
                      if ch % 2 == 0:
                          P.op(A, (lambda pv_, ch: lambda e: e.activation(out=vT[:, ch * 512:(ch + 1) * 512], in_=pv_, func=AF.Copy))(pv_, ch),
                               reads=[ps_key(pv_)], writes=["vT"])
                      else:
                          P.op(V, (lambda pv_, ch: lambda e: e.tensor_copy(out=vT[:, ch * 512:(ch + 1) * 512], in_=pv_))(pv_, ch),
                               reads=[ps_key(pv_)], writes=["vT"])
                  for ch in range(NQ // 512):
                      n_ = cn[0]
                      cn[0] += 1
                      pg = pv(accbanks[n_ % 3], 0, 512)
                      proj(pg, slot, 384, HALO + ch * 512, 512)
                      P.op(A, (lambda pg, ch: lambda e: e.activation(out=sgT[:, ch * 512:(ch + 1) * 512], in_=pg, func=AF.Silu))(pg, ch),
                           reads=[ps_key(pg)], writes=["sgT"])
                  if hp == 3:
                      load_wblock(woutd[:, :, 512:1024], 1, 512, gain=False)
                  stage_end("Dp")

                  tasks = []
                  kt_base = 0
                  for pi_, d in enumerate(PATTERNS):
                      nt = NQ // (128 * d)
                      for r in range(d):
                          for i in range(nt + 1):
                              tasks.append(dict(d=d, r=r, i=i, nt=nt, kti=kt_base + r * (nt + 1) + i, first=(pi_ == 0)))
                      kt_base += d * (nt + 1)
                  if ktlimit is not None:
                      tasks = tasks[:ktlimit]

                  def geom(tk_):
                      d, r, i, nt = tk_["d"], tk_["r"], tk_["i"], tk_["nt"]
                      base = HALO // d
                      m0 = base + 128 * i
                      jlo = max(i - 1, 0)
                      jhi = min(i, nt - 1)
                      qa = (jlo - (i - 1)) * 128
                      qb = (jhi - (i - 1) + 1) * 128
                      mq0 = (m0 - 128 + qa) - base
                      return d, r, i, nt, m0, jlo, jhi, qa, qb, mq0

                  def ph1(n, tk_):
                      d, r, i, nt, m0, jlo, jhi, qa, qb, mq0 = geom(tk_)
                      vs = n % 3
                      kcols = kT.rearrange("p (m d) -> p m d", d=d)[:, m0 - 64:m0 + 64, r]
                      vcols = vT.rearrange("p (m d) -> p m d", d=d)[:, m0 - 64:m0 + 64, r]
                      qcols = qT.rearrange("p (m d) -> p m d", d=d)[:, mq0:mq0 + (qb - qa), r]
                      ptx = PT16s[n % 2]
                      ptk = PT16k[n % 2]
                      P.op(T, lambda e: e.transpose(ptx[:, 0:128], vcols, ident_b), reads=["vT", "cb"], writes=[ptk])
                      if n % 2 == 0:
                          P.op(A, lambda e: e.activation(out=Vt[vs].rearrange("p (a b) -> p a b", b=64)[:, 0:3:2, :],
                                                         in_=ptx[:, 0:128].rearrange("p (a b) -> p a b", b=64), func=AF.Copy),
                               reads=[ptk], writes=[("Vt", vs)])
                      else:
                          P.op(V, lambda e: e.tensor_copy(out=Vt[vs].rearrange("p (a b) -> p a b", b=64)[:, 0:3:2, :],
                                                          in_=ptx[:, 0:128].rearrange("p (a b) -> p a b", b=64)),
                               reads=[ptk], writes=[("Vt", vs)])
                      sbank = 2 * (n % 2)
                      for h in range(2):
                          psh = PB[sbank + h][:, qa:qb]
                          P.op(T, (lambda psh, h: lambda e: e.matmul(psh, lhsT=kcols[64 * h:64 * h + 64], rhs=qcols[64 * h:64 * h + 64],
                                                                     start=True, stop=True))(psh, h),
                               reads=["kq0", "kq1"], writes=[("pb", sbank + h)])

                  def ph2(n, tk_):
                      d, r, i, nt, m0, jlo, jhi, qa, qb, mq0 = geom(tk_)
                      kti = tk_["kti"]
                      sbank = 2 * (n % 2)
                      ptt = PTt[n % 2].rearrange("p (h q) -> p h q", h=2)
                      psv = PSALL[:, 512 * sbank:512 * (sbank + 2)].rearrange("p (h q) -> p h q", h=2)
                      P.op(A, lambda e: e.activation(out=ptt[:, :, qa:qb], in_=psv[:, :, qa:qb], func=AF.Exp, bias=vbt[:, kti:kti + 1], scale=0.125),
                           reads=[("pb", sbank), ("pb", sbank + 1), "vbt"], writes=[("PT", n % 2)])
                      P.op(V, lambda e: e.tensor_tensor(
                          out=ptt[:, :, qa:qb], in0=ptt[:, :, qa:qb],
                          in1=band_b[:, qa:qb].unsqueeze(1).to_broadcast([128, 2, qb - qa]), op=ALU.mult),
                          reads=[("PT", n % 2), "cb"], writes=[("PT", n % 2)])

                  def ph3(n, tk_):
                      d, r, i, nt, m0, jlo, jhi, qa, qb, mq0 = geom(tk_)
                      vs = n % 3
                      first = tk_["first"]
                      ptt = PTt[n % 2].rearrange("p (h q) -> p h q", h=2)
                      merged = (d != 16) and (jlo == i - 1) and (jhi == i) and (jlo % 4 != 3)
                      if merged:
                          sl0 = jlo % 4
                          for h in range(2):
                              po = PB[5 + h][:, sl0 * 128:(sl0 + 2) * 128]
                              lw = Vt[vs][:, 0:128] if h == 0 else Vt[vs][:, 64:192]
                              P.op(T, (lambda po, lw, h: lambda e: e.matmul(po, lhsT=lw, rhs=ptt[:, h, 0:256], start=False, stop=True,
                                                                            skip_group_check=True))(po, lw, h),
                                   reads=[("Vt", vs), ("PT", n % 2)], writes=[("po", h, sl0), ("po", h, sl0 + 1)])
                      for j in range(jlo, jhi + 1):
                          ca = (j - (i - 1)) * 128
                          sl = (r % 4) if d == 16 else (j % 4)
                          if not merged:
                              for h in range(2):
                                  po = PB[5 + h][:, sl * 128:(sl + 1) * 128]
                                  lw = Vt[vs][:, 0:128] if h == 0 else Vt[vs][:, 64:192]
                                  st_flag = (j == i) if d == 16 else (j == i and sl == 0)
                                  P.op(T, (lambda po, lw, h, ca, j, st_flag: lambda e: e.matmul(po, lhsT=lw, rhs=ptt[:, h, ca:ca + 128],
                                                                                                start=st_flag, stop=(j == i - 1),
                                                                                                skip_group_check=True))(po, lw, h, ca, j, st_flag),
                                       reads=[("Vt", vs), ("PT", n % 2)], writes=[("po", h, sl)])
                          if j == i - 1:
                              keys4 = lambda h: [("po", h, s_) for s_ in range(4)]
                              if d == 16:
                                  if r % 4 == 3:
                                      grp = r // 4
                                      for h in range(2):
                                          acc = accA if h == 0 else accB
                                          src = PB[5 + h].rearrange("p (a m) -> p a m", a=4)
                                          dstv = acc.rearrange("p (m x) -> p m x", x=16)[:, :, 4 * grp:4 * grp + 4].rearrange("p m a -> p a m")
                                          ev(P, first, dstv, src, h, keys4(h))
                              elif j % 4 == 3:
                                  jg = j // 4
                                  for h in range(2):
                                      acc = accA if h == 0 else accB
                                      src = PB[5 + h].rearrange("p (a m) -> p a m", a=4)
                                      if d == 4:
                                          dstv = acc.rearrange("p (m x) -> p m x", x=4)[:, :, r].rearrange("p (a m) -> p a m", a=4)
                                      else:
                                          dstv = acc[:, jg * 512:(jg + 1) * 512].rearrange("p (a m) -> p a m", a=4)
                                      ev(P, first, dstv, src, h, keys4(h))

                  NTK = len(tasks)
                  for n in range(NTK + 2):
                      if 1 <= n <= NTK:
                          ph2(n - 1, tasks[n - 1])
                      if n < NTK:
                          ph1(n, tasks[n])
                      if n >= 2:
                          ph3(n - 2, tasks[n - 2])
                  stage_end("Da")
                  P.op(A, lambda e: e.activation(out=Dt[0:64, :], in_=accA[64:128, :], func=AF.Copy), reads=["accA"], writes=["Dt"])
                  P.op(A, lambda e: e.activation(out=Dt[64:128, :], in_=accB[0:64, :], func=AF.Copy), reads=["accB"], writes=["Dt"])
                  P.op(A, lambda e: e.activation(out=Dt, in_=Dt, func=AF.Ln), reads=["Dt"], writes=["Dt"])
                  P.op(A, lambda e: e.activation(out=Dt, in_=Dt, func=AF.Exp, scale=-1.0), reads=["Dt"], writes=["Dt"])
                  P.op(V, lambda e: e.tensor_tensor(out=Dt, in0=Dt, in1=sgT, op=ALU.mult), reads=["Dt", "sgT"], writes=["Dt"])
                  P.op(V, (lambda hp: lambda e: e.tensor_tensor(out=catT[0:64, hp, :], in0=accA[0:64, :], in1=Dt[0:64, :], op=ALU.mult))(hp),
                       reads=["accA", "Dt"], writes=["catT"])
                  P.op(V, (lambda hp: lambda e: e.tensor_tensor(out=catT[64:128, hp, :], in0=accB[64:128, :], in1=Dt[64:128, :], op=ALU.mult))(hp),
                       reads=["accB", "Dt"], writes=["catT"])
                  stage_end("Df")
              P.barrier(dummy)
              stage_end("D")

              e_steps = make_E(job)
              if job + 1 < njob:
                  A_cur = make_A(job + 1)
                  a_steps = list(A_cur["steps"])
              else:
                  a_steps = []
              for st_ in e_steps:
                  st_()
                  if a_steps:
                      a_steps.pop(0)()
              for st_ in a_steps:
                  st_()
        except _Stop:
            pass

        P.wait_all(S)
        P.emit(st)
    return nc


def ev(P, first, dstv, src, h, keys):
    key = "accA" if h == 0 else "accB"
    if first:
        P.op("vector", lambda e: e.tensor_copy(out=dstv, in_=src), reads=keys, writes=[key])
    else:
        P.op("vector", lambda e: e.tensor_tensor(out=dstv, in0=dstv, in1=src, op=ALU.add), reads=keys + [key], writes=[key])


def host_consts():
    cm = np.zeros((128, 640), np.float32)
    cm[:, 0:128] = np.eye(128, dtype=np.float32)
    perm = np.zeros((128, 128), np.float32)
    for m in range(128):
        c = m % 64
        if c < 8:
            perm[m + 8, m] = 1.0
        elif c < 16:
            perm[m - 8, m] = 1.0
    cm[:, 128:256] = perm
    p = np.arange(128)[:, None]
    j = np.arange(256)[None, :]
    cm[:, 256:512] = ((j >= p) & (j <= p + 128)).astype(np.float32)
    cm[:, 512:640] = 1.0 / 512.0
    return cm


def rope_consts():
    invf = np.zeros(128, np.float32)
    sgn = np.zeros(128, np.float32)
    inv = (500000.0 ** (-np.arange(8, dtype=np.float32) * 2.0 / 16.0)).astype(np.float32)
    for m in range(128):
        c = m % 64
        if c < 16:
            invf[m] = inv[c % 8]
            sgn[m] = -1.0 if c < 8 else 1.0
    return invf, sgn


def job_meta(seq_len, q_start):
    wpos = np.arange(q_start - HALO, q_start - HALO + NE)
    valid = (wpos >= 0) & (wpos < seq_len)
    pos = np.where(valid, wpos, 0).astype(np.float32)
    kv = np.zeros((128, NKT), np.float32)
    pidx = np.arange(128)
    for n, (d, r, i, nt) in enumerate(KT_LIST):
        base = HALO // d
        m = base - 64 + 128 * i + pidx
        e = d * m + r
        kv[:, n] = valid[e].astype(np.float32)
    return pos, kv, valid


def make_inputs(x_prompt, x_sample, norm_pre, w_in, conv_w, conv_b, conv_ln_g, conv_ln_b, w_pw, b_pw, w_out, norm_post, njob=NJOB):
    f = np.float32
    w_in = np.asarray(w_in, f)[0]
    cm = host_consts()
    invf, sgn = rope_consts()
    prm = np.zeros((128, 160), f)
    prm[:, 0:8] = np.asarray(norm_pre, f)[0].reshape(8, 128).T
    prm[:, 8:12] = np.asarray(conv_b, f)[0].reshape(4, 128).T
    prm[:, 12:16] = np.asarray(conv_ln_g, f)[0].reshape(4, 128).T
    prm[:, 16:20] = np.asarray(conv_ln_b, f)[0].reshape(4, 128).T
    prm[:, 20:24] = np.asarray(b_pw, f)[0].reshape(4, 128).T
    cw = np.asarray(conv_w, f)[0]
    prm[:, 24:148] = cw.T.reshape(4, 128, 31).transpose(1, 0, 2).reshape(128, 124)
    prm[:, 148] = invf
    prm[:, 149] = sgn
    prm[:, 150] = (invf.astype(np.float64) / (2.0 * np.pi)).astype(f)
    prm[:, 151] = (sgn.astype(np.float64) * 2.0 * np.pi).astype(f)
    gpost = np.ascontiguousarray(np.broadcast_to(np.asarray(norm_post, f)[0][None, :], (128, D)))
    wk = w_in.reshape(8, 128, 3584).transpose(1, 0, 2)
    wcv = np.stack([np.concatenate([wk[:, :, 2048 + g * 128:2048 + (g + 1) * 128],
                                    wk[:, :, 2560 + g * 128:2560 + (g + 1) * 128],
                                    wk[:, :, 3072 + g * 128:3072 + (g + 1) * 128]], axis=2) for g in range(4)])
    whp = np.stack([np.concatenate([wk[:, :, 0 + h * 128:0 + (h + 1) * 128],
                                    wk[:, :, 512 + h * 128:512 + (h + 1) * 128],
                                    wk[:, :, 1024 + h * 128:1024 + (h + 1) * 128],
                                    wk[:, :, 1536 + h * 128:1536 + (h + 1) * 128]], axis=2) for h in range(4)])
    wpw = np.ascontiguousarray(np.asarray(w_pw, f)[0].reshape(4, 128, 512).transpose(1, 0, 2))
    wout = np.ascontiguousarray(np.asarray(w_out, f)[0].reshape(8, 128, 1024).transpose(1, 0, 2))
    shared = {"cmat": cm, "prm": prm, "gpost": gpost, "wcv": np.ascontiguousarray(wcv), "whp": np.ascontiguousarray(whp),
              "wpw": wpw, "wout": wout}
    xp = np.asarray(x_prompt, f)
    xs = np.asarray(x_sample, f)[0]
    S_P = xp.shape[1]
    S_S = xs.shape[0]
    in_maps = []
    for c in range(NCORE):
        xw = np.zeros((njob, NE, D), f)
        pos = np.zeros((njob, NE), f)
        kval = np.zeros((njob, 128, NKT), f)
        jobs = [("p", c, 0), ("p", c, NQ), ("s", 0, c * NQ)][:njob]
        for jn, (kind, b, q0) in enumerate(jobs):
            src = xp[b] if kind == "p" else xs
            slen = S_P if kind == "p" else S_S
            lo = q0 - HALO
            hi = lo + NE
            a = max(lo, 0)
            bnd = min(hi, slen)
            xw[jn, a - lo:bnd - lo] = src[a:bnd]
            p_, kv_, _ = job_meta(slen, q0)
            pos[jn] = p_
            kval[jn] = kv_
        m = {"xw": xw, "pos": pos, "kval": kval}
        m.update(shared)
        in_maps.append(m)
    return in_maps


_NC_CACHE = {}


def kernel(x_prompt, x_sample, norm_pre, w_in, conv_w, conv_b, conv_ln_g, conv_ln_b, w_pw, b_pw, w_out, norm_post):
    in_maps = make_inputs(x_prompt, x_sample, norm_pre, w_in, conv_w, conv_b, conv_ln_g, conv_ln_b, w_pw, b_pw, w_out, norm_post)
    if "nc" not in _NC_CACHE:
        _NC_CACHE["nc"] = build_program()
    nc = _NC_CACHE["nc"]
    res = run_bass_kernel_spmd(nc, in_maps, core_ids=list(range(NCORE)))
    B, SP = np.asarray(x_prompt).shape[0:2]
    SS = np.asarray(x_sample).shape[1]
    y_prompt = np.zeros((B, SP, D), np.float32)
    y_sample = np.zeros((1, SS, D), np.float32)
    for c in range(NCORE):
        y = res.results[c]["y"]
        y_prompt[c, 0:NQ] = y[0]
        y_prompt[c, NQ:2 * NQ] = y[1]
        y_sample[0, c * NQ:(c + 1) * NQ] = y[2]
    return (y_prompt, y_sample)
```

```python
import math
from contextlib import ExitStack

import numpy as np
import concourse.bass as bass
import concourse.mybir as mybir
from concourse.bass_utils import run_bass_kernel_spmd

F32 = mybir.dt.float32
BF16 = mybir.dt.bfloat16
I32 = mybir.dt.int32
AF = mybir.ActivationFunctionType
ALU = mybir.AluOpType

D = 1024
NQ = 2048
HALO = 1024
NE = NQ + 2 * HALO
NJOB = 3
NCORE = 8
PATTERNS = (16, 4, 1)
NEG = -30000.0
TWO_PI = 2.0 * math.pi

ENGS = ["sync", "scalar", "vector", "gpsimd", "tensor"]


def key_tile_list():
    out = []
    for d in PATTERNS:
        nt = NQ // (128 * d)
        for r in range(d):
            for i in range(nt + 1):
                out.append((d, r, i, nt))
    return out


def zero_window_tiles(job):
    if job == 0:
        return set(range(0, HALO // 128))
    if job == 1:
        return set(range((NE - HALO) // 128, NE // 128))
    return set()


KT_LIST = key_tile_list()
NKT = len(KT_LIST)


class Prog:
    def __init__(self, nc):
        self.nc = nc
        self.lists = {e: [] for e in ENGS}
        self.cnt = {e: 0 for e in ENGS}
        self.waited = {e: {} for e in ENGS}
        self.res = {}
        self.sems = {}
        self.dma_sems = {}
        self.dma_cnt = {}
        self.forced = {e: {} for e in ENGS}

    def _deps(self, eng, reads, writes, is_dma=False):
        need = dict(self.forced[eng])
        self.forced[eng] = {}
        skip_self = (not is_dma) and eng != "gpsimd"

        def add(k, v):
            if need.get(k, 0) < v:
                need[k] = v

        for r in reads:
            st = self.res.get(r)
            if st:
                for k, v in st["w"].items():
                    add(k, v)
        for w in writes:
            st = self.res.get(w)
            if st:
                for k, v in st["w"].items():
                    if not (skip_self and k == eng):
                        add(k, v)
                for k, v in st["r"].items():
                    if not (skip_self and k == eng):
                        add(k, v)
        waits = []
        for k, v in need.items():
            if k == "tensor" and eng == "tensor":
                continue
            if self.waited[eng].get(k, 0) < v:
                self.waited[eng][k] = v
                waits.append((k, v))
        return waits

    def _mark(self, reads, writes, tok):
        for r in reads:
            st = self.res.setdefault(r, {"w": {}, "r": {}})
            st["r"][tok[0]] = tok[1]
        for w in writes:
            st = self.res.setdefault(w, {"w": {}, "r": {}})
            st["w"][tok[0]] = tok[1]

    def op(self, eng, fn, reads=(), writes=()):
        waits = self._deps(eng, reads, writes)
        self.cnt[eng] += 1
        tok = (eng, self.cnt[eng])
        self.lists[eng].append(("op", waits, fn, None))
        self._mark(reads, writes, tok)
        return tok

    def dma(self, queue, out, in_, semkey, reads=(), writes=(), **kw):
        waits = self._deps(queue, reads, writes, is_dma=True)
        self.dma_cnt[semkey] = self.dma_cnt.get(semkey, 0) + 16
        tok = (("dma", semkey), self.dma_cnt[semkey])
        self.lists[queue].append(("dma", waits, (out, in_, kw), semkey))
        self._mark(reads, writes, tok)
        return tok

    def barrier(self, dummy):
        waits = [(e, self.cnt[e]) for e in ENGS if e != "gpsimd" and self.cnt[e] > 0]
        waits += [(("dma", k), v) for k, v in self.dma_cnt.items()]
        waits.append(("gpsimd", self.cnt["gpsimd"]))
        w2 = []
        for k, v in waits:
            if v > 0 and self.waited["gpsimd"].get(k, 0) < v:
                self.waited["gpsimd"][k] = v
                w2.append((k, v))
        self.cnt["gpsimd"] += 1
        tok = ("gpsimd", self.cnt["gpsimd"])
        self.lists["gpsimd"].append(("op", w2, lambda e: e.memset(dummy, 0.0), None))
        for e in ENGS:
            self.forced[e]["gpsimd"] = tok[1]

    def wait_all(self, eng):
        waits = [(("dma", k), v) for k, v in self.dma_cnt.items()]
        waits += [(e, self.cnt[e]) for e in ENGS if self.cnt[e] > 0 and e != eng]
        self.lists[eng].append(("wait", waits, None, None))

    def emit(self, stack):
        nc = self.nc
        for e in ENGS:
            self.sems[e] = stack.enter_context(nc.semaphore("s_" + e))
        for k in self.dma_cnt:
            self.dma_sems[k] = stack.enter_context(nc.semaphore("d_" + str(k)))
        block = stack.enter_context(nc.Block())

        def semof(k):
            if isinstance(k, tuple):
                return self.dma_sems[k[1]]
            return self.sems[k]

        def run(ename):
            def body(engine):
                for kind, waits, payload, semkey in self.lists[ename]:
                    for k, v in waits:
                        engine.wait_ge(semof(k), v)
                    if kind == "op":
                        payload(engine).then_inc(self.sems[ename], 1)
                    elif kind == "dma":
                        out, in_, kw = payload
                        engine.dma_start(out=out, in_=in_, **kw).then_inc(self.dma_sems[semkey], 16)
            return body

        block.sync(run("sync"))
        block.scalar(run("scalar"))
        block.vector(run("vector"))
        block.gpsimd(run("gpsimd"))
        block.tensor(run("tensor"))


_LASTP = [None]


class _Stop(Exception):
    pass


def build_program(njob=NJOB, dbg=None, upto=None, ktlimit=None, sub=None):
    nc = bass.Bass("TRN2", target_bir_lowering=False)
    dt = nc.dram_tensor
    xw = dt("xw", [njob, NE, D], F32, kind="ExternalInput").ap()
    posd = dt("pos", [njob, NE], F32, kind="ExternalInput").ap()
    kvald = dt("kval", [njob, 128, NKT], F32, kind="ExternalInput").ap()
    cmatd = dt("cmat", [128, 640], F32, kind="ExternalInput").ap()
    prmd = dt("prm", [128, 160], F32, kind="ExternalInput").ap()
    gpostd = dt("gpost", [128, D], F32, kind="ExternalInput").ap()
    wcvd = dt("wcv", [4, 128, 8, 384], F32, kind="ExternalInput").ap()
    whpd = dt("whp", [4, 128, 8, 512], F32, kind="ExternalInput").ap()
    wpwd = dt("wpw", [128, 4, 512], F32, kind="ExternalInput").ap()
    woutd = dt("wout", [128, 8, 1024], F32, kind="ExternalInput").ap()
    yd = dt("y", [njob, NQ, D], F32, kind="ExternalOutput").ap()
    dbg_out = {}
    if dbg:
        for name, shape in dbg.items():
            dbg_out[name] = dt("dbg_" + name, list(shape), F32, kind="ExternalOutput").ap()

    st = ExitStack()
    with st:
        sb = lambda name, shape, dtype: st.enter_context(nc.sbuf_tensor("s_" + name, shape, dtype))
        hT = sb("hT", [128, 8, NE], BF16)
        catT2 = sb("catT", [128, 8 * NQ], BF16)
        catT = catT2[:, :].rearrange("p (k n) -> p k n", k=8)
        XR = sb("XR", [128, 12288], F32)
        ropeC = sb("ropeC", [128, NE], BF16)
        ropeS = sb("ropeS", [128, NE], BF16)
        wst = [sb("wst%d" % i, [128, 2, 512], F32) for i in range(2)]
        wblk = [sb("wblk%d" % i, [128, 8, 512], BF16) for i in range(2)]
        wpw = sb("wpw", [128, 4, 512], BF16)
        cb = sb("cb", [128, 640], BF16)
        prm = sb("prm", [128, 160], F32)
        vbt = sb("vbt", [128, NKT], F32)
        small = sb("small", [128, 16], F32)
        small2 = sb("small2", [128, 32], F32)
        SCR = sb("SCR", [128, 4096], F32)
        PSALL = st.enter_context(nc.psum_tensor("psall", [128, 4096], F32))
        PB = [PSALL[:, 512 * i:512 * (i + 1)] for i in range(8)]
        PT16 = PSALL[:, 3584:4096].bitcast(BF16)

        P = Prog(nc)
        _LASTP[0] = P
        V, A, G, T, S = "vector", "scalar", "gpsimd", "tensor", "sync"
        cf = XR[:, 8192:8832]

        ident_b = cb[:, 0:128]
        perm_b = cb[:, 128:256]
        band_b = cb[:, 256:512]
        onesm_b = cb[:, 512:640]
        gpre = prm[:, 0:8]
        invf = prm[:, 148:149]
        sgn = prm[:, 149:150]
        eps_rms = small[:, 0:1]
        eps_ln = small[:, 1:2]
        dummy = small[:, 8:9]

        P.dma(S, cf[:], cmatd[:, :], "c0", writes=["cf"])
        P.dma(S, prm[:], prmd[:, :], "c1", writes=["prm"])
        P.op(V, lambda e: e.tensor_copy(out=cb[:], in_=cf[:]), reads=["cf"], writes=["cb"])
        P.op(G, lambda e: e.memset(small[:, 0:1], 1e-6), writes=["small"])
        P.op(G, lambda e: e.memset(small[:, 1:2], 1e-5), writes=["small"])
        for hlf in range(2):
            P.dma(S, wst[hlf][:], wpwd[:, 2 * hlf:2 * hlf + 2, :], "wst%d" % hlf, writes=[("wst", hlf)])
            P.op(G, (lambda h: lambda e: e.tensor_copy(out=wpw[:, 2 * h:2 * h + 2, :], in_=wst[h][:]))(hlf),
                 reads=[("wst", hlf)], writes=["wpw"])

        wst_ctr = [0]

        def load_wblock(src, slot, W, gain=True):
            for q4 in range(4):
                s = wst_ctr[0] % 2
                wst_ctr[0] += 1
                P.dma(S, wst[s][:, :, 0:W], src[:, 2 * q4:2 * q4 + 2, :], "wst%d" % s, writes=[("wst", s)])
                if gain:
                    P.op(V, (lambda s, q4: lambda e: e.tensor_tensor(
                        out=wblk[slot][:, 2 * q4:2 * q4 + 2, 0:W], in0=wst[s][:, :, 0:W],
                        in1=gpre[:, 2 * q4:2 * q4 + 2].unsqueeze(2).to_broadcast([128, 2, W]), op=ALU.mult))(s, q4),
                        reads=[("wst", s), "prm"], writes=[("wblk", slot)])
                else:
                    P.op(V, (lambda s, q4: lambda e: e.tensor_copy(
                        out=wblk[slot][:, 2 * q4:2 * q4 + 2, 0:W], in_=wst[s][:, :, 0:W]))(s, q4),
                        reads=[("wst", s)], writes=[("wblk", slot)])

        def proj(ps, slot, c0, t0, n, first_reads=()):
            hkeys = [("hT", tt) for tt in range(t0 // 128, (t0 + n - 1) // 128 + 1)]
            for kc in range(8):
                P.op(T, (lambda kc: lambda e: e.matmul(ps, lhsT=wblk[slot][:, kc, c0:c0 + 128], rhs=hT[:, kc, t0:t0 + n],
                                                      start=(kc == 0), stop=(kc == 7)))(kc),
                     reads=[("wblk", slot)] + hkeys, writes=[ps_key(ps)])

        psk = {}

        def ps_key(ap):
            return psk[id(ap)]

        def pv(bank, a, b, key=None):
            ap = PB[bank][:, a:b]
            psk[id(ap)] = key if key is not None else ("pb", bank)
            return ap

        def dump(name, ap_src, rows, reads):
            if dbg and name in dbg_out:
                P.dma(S, dbg_out[name], ap_src, "dbg", reads=reads)

        def stage_end(name):
            if upto == name:
                raise _Stop()

        PIS = 3.1415925
        PT16s = [PB[7 - 0] if False else PSALL[:, 3584:4096].bitcast(BF16), PSALL[:, 2048:2560].bitcast(BF16)]
        PT16k = [("pb", 7), ("pb", 4)]
        PT16b6 = PSALL[:, 3072:3584].bitcast(BF16)
        def make_A(job):
            xt = [SCR[:, 0:1024], SCR[:, 1024:2048], SCR[:, 2048:3072]]
            hb = [SCR[:, 3072:3584].bitcast(BF16), SCR[:, 3584:4096].bitcast(BF16)]
            ptA = [PT16s[0], PT16b6]
            ptAk = [("pb", 7), ("pb", 6)]
            NT = NE // 128

            acnt = {}

            def a1_dma(t):
                acnt[t] = len(acnt)
                x3 = acnt[t] % 3
                P.dma(S, xt[x3], xw[job, t * 128:(t + 1) * 128, :], "x%d" % x3, writes=[("xt", x3)])

            def a1_norm(t):
                s = acnt[t] % 2
                x3 = acnt[t] % 3
                ss = small[:, 2 + s:3 + s]
                P.op(A, lambda e: e.activation(out=hb[s], in_=xt[x3], func=AF.Square, accum_out=ss), reads=[("xt", x3)], writes=[("hb", s), ("ss", s)])
                P.op(A, lambda e: e.activation(out=ss, in_=ss, func=AF.Sqrt, bias=eps_rms, scale=1.0 / D), reads=[("ss", s), "small"], writes=[("ss", s)])
                P.op(V, lambda e: e.reciprocal(out=ss, in_=ss), reads=[("ss", s)], writes=[("ss", s)])
                P.op(V, lambda e: e.tensor_scalar(out=hb[s], in0=xt[x3], scalar1=ss, scalar2=None, op0=ALU.mult),
                     reads=[("xt", x3), ("ss", s)], writes=[("hb", s)])

            def a1(t):
                a1_dma(t)
                a1_norm(t)


            def a2(t):
                s = acnt[t] % 2
                for kc in range(8):
                    P.op(T, (lambda kc: lambda e: e.transpose(ptA[s][:, kc * 128:(kc + 1) * 128], hb[s][:, kc * 128:(kc + 1) * 128], ident_b))(kc),
                         reads=[("hb", s), "cb"], writes=[ptAk[s]])
                P.op(V, lambda e: e.tensor_copy(out=hT[:, :, t * 128:(t + 1) * 128], in_=ptA[s].rearrange("p (k n) -> p k n", n=128)),
                     reads=[ptAk[s]], writes=[("hT", t)])

            zt = zero_window_tiles(job)
            first_tiles = [t for t in range(7, 25) if t not in zt]
            deferred = [t for t in range(NT) if t not in first_tiles and t not in zt]
            for t in sorted(zt):
                if 7 <= t < 25:
                    P.op(G, (lambda t: lambda e: e.memset(hT[:, :, t * 128:(t + 1) * 128], 0.0))(t), writes=[("hT", t)])
            RPN = 512
            rbase = 9216
            rsets = [(XR[:, rbase + 3 * RPN * i:rbase + 3 * RPN * i + RPN], XR[:, rbase + 3 * RPN * i + RPN:rbase + 3 * RPN * i + 2 * RPN],
                      XR[:, rbase + 3 * RPN * i + 2 * RPN:rbase + 3 * RPN * i + 3 * RPN]) for i in range(2)]
            halfpi = small[:, 9:10]
            P.op(G, lambda e: e.memset(halfpi, 0.5 * math.pi), writes=["small"])

            def rope_a(k):
                rp_pos, rp_s, rp_a = rsets[k % 2]
                kp, ks, ka = ("posf", k % 2), ("s1", k % 2), ("ra", k % 2)
                rp_si = rp_s.bitcast(I32)
                c0 = RPN * k
                P.dma(S, rp_pos, posd[job, c0:c0 + RPN].partition_broadcast(128), "r%d" % (k % 2), writes=[kp])
                P.op(V, lambda e: e.tensor_scalar(out=rp_pos, in0=rp_pos, scalar1=prm[:, 150:151], scalar2=None, op0=ALU.mult), reads=[kp, "prm"], writes=[kp])
                P.op(V, lambda e: e.tensor_copy(out=rp_si, in_=rp_pos), reads=[kp], writes=[ks, ka])
                P.op(V, lambda e: e.tensor_copy(out=rp_s, in_=rp_si), reads=[ks], writes=[ks])
                P.op(V, lambda e: e.tensor_tensor(out=rp_s, in0=rp_pos, in1=rp_s, op=ALU.subtract), reads=[ks, kp], writes=[ks])
                P.op(V, lambda e: e.tensor_scalar(out=rp_s, in0=rp_s, scalar1=-0.4999999, scalar2=0.4999999, op0=ALU.max, op1=ALU.min), reads=[ks], writes=[ks])

            def rope_b(k):
                rp_pos, rp_s, rp_a = rsets[k % 2]
                ks, ka = ("s1", k % 2), ("ra", k % 2)
                c0 = RPN * k
                P.op(A, lambda e: e.activation(out=rp_a, in_=rp_s, func=AF.Sin, scale=math.pi), reads=[ks], writes=[ka])
                P.op(A, lambda e: e.activation(out=ropeS[:, c0:c0 + RPN], in_=rp_s, func=AF.Sin, scale=prm[:, 151:152]), reads=[ks, "prm"], writes=["ropeS"])

            def rope_c(k):
                rp_pos, rp_s, rp_a = rsets[k % 2]
                ks, ka = ("s1", k % 2), ("ra", k % 2)
                c0 = RPN * k
                P.op(V, lambda e: e.tensor_tensor(out=rp_a, in0=rp_a, in1=rp_a, op=ALU.mult), reads=[ka], writes=[ka])
                P.op(V, lambda e: e.tensor_scalar(out=ropeC[:, c0:c0 + RPN], in0=rp_a, scalar1=-2.0, scalar2=1.0, op0=ALU.mult, op1=ALU.add),
                     reads=[ka], writes=["ropeC"])

            NRP = NE // RPN
            rstep = [0]

            def rope_step():
                r = rstep[0]
                rstep[0] += 1
                if 0 <= r - 2 < NRP:
                    rope_c(r - 2)
                if 0 <= r - 1 < NRP:
                    rope_b(r - 1)
                if r < NRP:
                    rope_a(r)

            steps = []

            def mk(n_, t):
                def f():
                    if t is not None:
                        a1(t)
                    if n_ >= 1:
                        a2(first_tiles[n_ - 1])
                    pass
                return f

            for n_, t in enumerate(first_tiles + [None]):
                steps.append(mk(n_, t))
            return dict(steps=steps, a1_dma=a1_dma, a1_norm=a1_norm, a2=a2, deferred=deferred)

        def make_E(job):
            gpb = XR[:, 0:1024]
            xt2s = [XR[:, 1024 * (1 + i):1024 * (2 + i)] for i in range(4)]
            ots = [XR[:, 5120:6144], XR[:, 6144:7168]]
            P.dma(S, gpb, gpostd[:, :], "c3", writes=["gpb"])
            junk2s = [XR[:, 7168:7680].bitcast(BF16), XR[:, 7680:8192].bitcast(BF16)]
            pys = {}
            ssE0 = small2[:, 0:16]
            ssE1 = small2[:, 16:32]

            def e1(t):
                s = t % 2
                x4 = t % 4
                P.dma(S, xt2s[x4], xw[job, HALO + t * 128:HALO + (t + 1) * 128, :], "xe%d" % x4, writes=[("xt2", x4)])
                b4 = 2 * (t % 3)
                py = [pv(b4, 0, 512), pv(b4 + 1, 0, 512)]
                pys[t] = py
                for nh in range(2):
                    for kc in range(8):
                        P.op(T, (lambda nh, kc: lambda e: e.matmul(py[nh], lhsT=catT[:, kc, t * 128:(t + 1) * 128], rhs=wblk[nh][:, kc, :],
                                                                   start=(kc == 0), stop=(kc == 7)))(nh, kc),
                             reads=["catT", ("wblk", nh)], writes=[("pb", b4 + nh)])
                P.op(A, lambda e: e.activation(out=junk2s[s][:, 0:1024], in_=PSALL[:, 512 * b4:512 * (b4 + 2)], func=AF.Square,
                                               accum_out=ssE0[:, t:t + 1]),
                     reads=[("pb", b4), ("pb", b4 + 1)], writes=[("junk2", s), ("ssE", t // 2)])

            def e2(t):
                s = t % 2
                py = pys[t]
                ot = ots[s]
                xt2 = xt2s[t % 4]
                k = ("ssE", t // 2)
                a0 = ssE0[:, t:t + 1]
                P.op(A, lambda e: e.activation(out=a0, in_=a0, func=AF.Sqrt, bias=eps_rms, scale=1.0 / D), reads=[k, "small"], writes=[k])
                P.op(V, lambda e: e.reciprocal(out=a0, in_=a0), reads=[k], writes=[k])
                for nh in range(2):
                    P.op(V, (lambda nh: lambda e: e.scalar_tensor_tensor(out=ot[:, nh * 512:(nh + 1) * 512], in0=py[nh], scalar=a0,
                                                                         in1=gpb[:, nh * 512:(nh + 1) * 512], op0=ALU.mult, op1=ALU.mult))(nh),
                         reads=[("pb", 2 * (t % 3) + nh), k, "gpb"], writes=[("ot", s)])
                P.op(V, lambda e: e.tensor_tensor(out=ot, in0=ot, in1=xt2, op=ALU.add), reads=[("ot", s), ("xt2", t % 4)], writes=[("ot", s)])
                P.dma(G, yd[job, t * 128:(t + 1) * 128, :], ot, "st%d" % s, reads=[("ot", s)])


            NTE = NQ // 128
            steps = []

            def mk(t):
                def f():
                    if t < NTE:
                        e1(t)
                    if t >= 1:
                        e2(t - 1)
                return f

            for t in range(NTE + 1):
                steps.append(mk(t))
            return steps

        try:
          A_cur = make_A(0)
          for st_ in A_cur["steps"]:
              st_()
          for job in range(njob):
              a1_dma, a1_norm, a2, deferred = A_cur["a1_dma"], A_cur["a1_norm"], A_cur["a2"], A_cur["deferred"]
              load_wblock(wcvd[0], 0, 384)
              P.dma(S, vbt[:], kvald[job], "c2", writes=["vbt"])
              P.op(V, lambda e: e.tensor_scalar(out=vbt[:], in0=vbt[:], scalar1=-1.0, scalar2=-NEG, op0=ALU.add, op1=ALU.mult),
                   reads=["vbt"], writes=["vbt"])


              P.barrier(dummy)
              stage_end("A")

              uT = XR[:, 0:4160].bitcast(BF16).rearrange("p (g n) -> p g n", g=4)
              diag = XR[:, 4160:4160 + 7936].bitcast(BF16).rearrange("p (g j n) -> p g j n", g=4, j=31)
              cscr0 = catT2[:, 0:4 * NQ].bitcast(F32)
              sigts = [cscr0[:, 0:512], cscr0[:, 512:1024]]
              sigts2 = [cscr0[:, 1536:2048], cscr0[:, 2048:2560]]
              dq = list(deferred)
              dstep = [0]
              u0 = HALO - 16
              pieces = [(u0 + 512 * c, 512) for c in range(4)] + [(u0 + 2048, 32)]
              pn = 0
              for g in range(4):
                  slot = g % 2
                  if g < 3:
                      load_wblock(wcvd[g + 1], (g + 1) % 2, 384)
                  else:
                      load_wblock(whpd[0], 0, 512)
                  for pidx, (t0, n) in enumerate(pieces):
                      if pidx == 2:
                          P.op(V, (lambda gg: lambda e: e.tensor_tensor(
                              out=diag[:, gg, :, :], in0=ident_b.unsqueeze(1).to_broadcast([128, 31, 128]),
                              in1=prm[:, 24 + gg * 31:24 + gg * 31 + 31].unsqueeze(2).to_broadcast([128, 31, 128]), op=ALU.mult))(g),
                              reads=["cb", "prm"], writes=["diag"])
                      bs = 3 * (pn % 2)
                      sg_ = sigts[pn % 2]
                      sk = ("sigt", pn % 2)
                      pn += 1
                      pa = pv(bs + 0, 0, n)
                      pb_ = pv(bs + 1, 0, n)
                      proj(pa, slot, 0, t0, n)
                      proj(pb_, slot, 128, t0, n)
                      P.op(A, (lambda pb_, n, sg_: lambda e: e.activation(out=sg_[:, 0:n], in_=pb_, func=AF.Sigmoid))(pb_, n, sg_),
                           reads=[ps_key(pb_)], writes=[sk])
                      P.op(V, (lambda pa, n, g, t0, sg_: lambda e: e.tensor_tensor(out=uT[:, g, t0 - u0:t0 - u0 + n], in0=pa, in1=sg_[:, 0:n],
                                                                                    op=ALU.mult))(pa, n, g, t0, sg_),
                           reads=[ps_key(pa), sk], writes=["uT"])
                      if n == 512:
                          pg = pv(bs + 2, 0, 512)
                          proj(pg, slot, 256, t0 + 16, 512)
                          q0 = t0 + 16 - HALO
                          sg2 = sigts2[pn % 2]
                          sk2 = ("sigt2", pn % 2)
                          P.op(A, (lambda pg, sg2: lambda e: e.activation(out=sg2, in_=pg, func=AF.Sigmoid))(pg, sg2),
                               reads=[ps_key(pg)], writes=[sk2])
                          P.op(V, (lambda pg, g, q0, sg2: lambda e: e.tensor_tensor(out=catT[:, 4 + g, q0:q0 + 512], in0=pg, in1=sg2, op=ALU.mult))(pg, g, q0, sg2),
                               reads=[ps_key(pg), sk2], writes=["catT"])
                      nsteps = 1
                      for _ in range(nsteps):
                          kq = dstep[0]
                          dstep[0] += 1
                          if 0 <= kq - 2 < len(dq):
                              a2(dq[kq - 2])
                          if 0 <= kq - 1 < len(dq):
                              a1_norm(dq[kq - 1])
                          if kq < len(dq):
                              a1_dma(dq[kq])
              while dstep[0] - 2 < len(dq):
                  kq = dstep[0]
                  dstep[0] += 1
                  if 0 <= kq - 2 < len(dq):
                      a2(dq[kq - 2])
                  if 0 <= kq - 1 < len(dq):
                      a1_norm(dq[kq - 1])
                  if kq < len(dq):
                      a1_dma(dq[kq])
              P.barrier(dummy)
              stage_end("C1")
              CW = 512
              NCH = NQ // CW
              cscr = catT2[:, 0:4 * NQ].bitcast(F32)
              co = cscr[:, 0:2048].rearrange("p (g n) -> p g n", g=4)
              cob = cscr[:, 2048:3072].bitcast(BF16).rearrange("p (g n) -> p g n", g=4)
              sqb = cscr[:, 3072:4096].bitcast(BF16).rearrange("p (g n) -> p g n", g=4)
              slb = SCR[:, 0:1024].bitcast(BF16).rearrange("p (g n) -> p g n", g=4)
              m2 = SCR[:, 1024:1536]
              rstd = SCR[:, 1536:2048]
              nmr = SCR[:, 2048:2560]
              pcs = [pv(g, 0, CW) for g in range(4)]
              pm = pv(4, 0, CW)
              pq = pv(5, 0, CW)
              pws = [pv(6, 0, CW), pv(7, 0, CW)]

              def c2_conv(c, groups=(0, 1, 2, 3)):
                  q0 = c * CW
                  for g in groups:
                      for j in range(31):
                          off = q0 + 1 + j
                          P.op(T, (lambda g, j, off: lambda e: e.matmul(pcs[g], lhsT=diag[:, g, j, :], rhs=uT[:, g, off:off + CW],
                                                                        start=(j == 0), stop=(j == 30)))(g, j, off),
                               reads=["diag", "uT"], writes=[("pb", g)])

              def c2_evac(c):
                  for g in range(4):
                      bia = prm[:, 8 + g:9 + g]
                      P.op(A, (lambda g, bia: lambda e: e.activation(out=co[:, g, :], in_=pcs[g], func=AF.Identity, bias=bia, scale=1.0))(g, bia),
                           reads=[("pb", g), "prm"], writes=[("co", g)])
                      P.op(A, (lambda g, bia: lambda e: e.activation(out=cob[:, g, :], in_=pcs[g], func=AF.Identity, bias=bia, scale=1.0))(g, bia),
                           reads=[("pb", g), "prm"], writes=[("cob", g)])
                      P.op(A, (lambda g, bia: lambda e: e.activation(out=sqb[:, g, :], in_=pcs[g], func=AF.Square, bias=bia, scale=1.0))(g, bia),
                           reads=[("pb", g), "prm"], writes=[("sqb", g)])

              def c2_stats(c):
                  for g in range(4):
                      P.op(T, (lambda g: lambda e: e.matmul(pm, lhsT=onesm_b, rhs=cob[:, g, :], start=(g == 0), stop=(g == 3)))(g),
                           reads=["cb", ("cob", g)], writes=[("pb", 4)])
                  for g in range(4):
                      P.op(T, (lambda g: lambda e: e.matmul(pq, lhsT=onesm_b, rhs=sqb[:, g, :], start=(g == 0), stop=(g == 3)))(g),
                           reads=["cb", ("sqb", g)], writes=[("pb", 5)])

              def c2_rest(c):
                  q0 = c * CW
                  P.op(A, lambda e: e.activation(out=m2, in_=pm, func=AF.Square), reads=[("pb", 4)], writes=["m2"])
                  P.op(V, lambda e: e.tensor_tensor(out=m2, in0=pq, in1=m2, op=ALU.subtract), reads=[("pb", 5), "m2"], writes=["m2"])
                  P.op(A, lambda e: e.activation(out=rstd, in_=m2, func=AF.Sqrt, bias=eps_ln, scale=1.0), reads=["m2", "small"], writes=["rstd"])
                  P.op(V, lambda e: e.reciprocal(out=rstd, in_=rstd), reads=["rstd"], writes=["rstd"])
                  P.op(V, lambda e: e.scalar_tensor_tensor(out=nmr, in0=pm, scalar=-1.0, in1=rstd, op0=ALU.mult, op1=ALU.mult),
                       reads=[("pb", 4), "rstd"], writes=["nmr"])
                  for g in range(4):
                      P.op(V, (lambda g: lambda e: e.tensor_tensor(out=co[:, g, :], in0=co[:, g, :], in1=rstd, op=ALU.mult))(g),
                           reads=[("co", g), "rstd"], writes=[("co", g)])
                      P.op(V, (lambda g: lambda e: e.tensor_tensor(out=co[:, g, :], in0=co[:, g, :], in1=nmr, op=ALU.add))(g),
                           reads=[("co", g), "nmr"], writes=[("co", g)])
                      P.op(A, (lambda g: lambda e: e.activation(out=slb[:, g, :], in_=co[:, g, :], func=AF.Silu,
                                                                bias=prm[:, 16 + g:17 + g], scale=prm[:, 12 + g:13 + g]))(g),
                           reads=[("co", g), "prm"], writes=[("slb", g)])
                  for go in range(4):
                      pw_ = pws[go % 2]
                      for gi in range(4):
                          P.op(T, (lambda pw_, go, gi: lambda e: e.matmul(pw_, lhsT=wpw[:, gi, go * 128:(go + 1) * 128], rhs=slb[:, gi, :],
                                                                          start=(gi == 0), stop=(gi == 3)))(pw_, go, gi),
                               reads=["wpw", ("slb", gi)], writes=[("pb", 6 + go % 2)])
                      P.op(V, (lambda pw_, go, q0: lambda e: e.scalar_tensor_tensor(
                          out=catT[:, 4 + go, q0:q0 + CW], in0=pw_, scalar=prm[:, 20 + go:21 + go], in1=catT[:, 4 + go, q0:q0 + CW],
                          op0=ALU.add, op1=ALU.mult))(pw_, go, q0),
                          reads=[("pb", 6 + go % 2), "prm", "catT"], writes=["catT"])

              RP2 = 512
              q_pos, q_s, q_a = SCR[:, 2560:3072], SCR[:, 3072:3584], SCR[:, 3584:4096]
              q_si = q_s.bitcast(I32)

              def rp_dma(k):
                  P.dma(S, q_pos, posd[job, RP2 * k:RP2 * (k + 1)].partition_broadcast(128), "r0", writes=["qpos"])

              def rp_a(k):
                  P.op(V, lambda e: e.tensor_scalar(out=q_pos, in0=q_pos, scalar1=prm[:, 150:151], scalar2=None, op0=ALU.mult), reads=["qpos", "prm"], writes=["qpos"])
                  P.op(V, lambda e: e.tensor_copy(out=q_si, in_=q_pos), reads=["qpos"], writes=["qs", "qa"])
                  P.op(V, lambda e: e.tensor_copy(out=q_s, in_=q_si), reads=["qs"], writes=["qs"])
                  P.op(V, lambda e: e.tensor_tensor(out=q_s, in0=q_pos, in1=q_s, op=ALU.subtract), reads=["qs", "qpos"], writes=["qs"])
                  P.op(V, lambda e: e.tensor_scalar(out=q_s, in0=q_s, scalar1=-0.4999999, scalar2=0.4999999, op0=ALU.max, op1=ALU.min), reads=["qs"], writes=["qs"])
                  if k + 1 < NE // RP2:
                      rp_dma(k + 1)

              def rp_b(k):
                  c0 = RP2 * k
                  P.op(A, lambda e: e.activation(out=q_a, in_=q_s, func=AF.Sin, scale=math.pi), reads=["qs"], writes=["qa"])
                  P.op(A, lambda e: e.activation(out=ropeS[:, c0:c0 + RP2], in_=q_s, func=AF.Sin, scale=prm[:, 151:152]), reads=["qs", "prm"], writes=["ropeS"])

              def rp_c(k):
                  c0 = RP2 * k
                  P.op(V, lambda e: e.tensor_tensor(out=q_a, in0=q_a, in1=q_a, op=ALU.mult), reads=["qa"], writes=["qa"])
                  P.op(V, lambda e: e.tensor_scalar(out=ropeC[:, c0:c0 + RP2], in0=q_a, scalar1=-2.0, scalar2=1.0, op0=ALU.mult, op1=ALU.add),
                       reads=["qa"], writes=["ropeC"])

              rp_dma(0)
              c2_conv(0)
              for c in range(NCH):
                  c2_evac(c)
                  rp_a(2 * c)
                  if c + 1 < NCH:
                      c2_conv(c + 1, groups=(0,))
                  rp_b(2 * c)
                  c2_stats(c)
                  rp_c(2 * c)
                  rp_a(2 * c + 1)
                  if c + 1 < NCH:
                      c2_conv(c + 1, groups=(1, 2, 3))
                  rp_b(2 * c + 1)
                  c2_rest(c)
                  rp_c(2 * c + 1)
              c2keys = ["uT", "diag", "m2", "rstd", "nmr", "qpos", "qs", "qa"] + [(nm, g) for nm in ("co", "cob", "sqb", "slb") for g in range(4)]
              dkeys = ["kq0", "kq1", "vT", "sgT", "accA", "accB", "Dt", "catT", ("PT", 0), ("PT", 1), ("Vt", 0), ("Vt", 1), ("Vt", 2),
                       ("kraw", 0), ("kraw", 1), ("t1", 0), ("t1", 1)]
              P.op(G, lambda e: e.memset(dummy, 0.0), reads=[], writes=c2keys + dkeys)
              stage_end("C2")

              XB = XR[:, 0:6144].bitcast(BF16)
              kT = XB[:, 0:4096]
              vT = XB[:, 4096:8192]
              qT = XB[:, 8192:10240]
              sgT = XB[:, 10240:12288]
              accA = XR[:, 6144:8192]
              accB = XR[:, 8192:10240]
              Dt = XR[:, 10240:12288]
              kraws = [SCR[:, 2560:2816].bitcast(BF16), SCR[:, 2816:3072].bitcast(BF16)]
              t1s = [SCR[:, 3072:3584], SCR[:, 3584:4096]]
              PTt = [SCR[:, 0:256].bitcast(BF16), SCR[:, 256:512].bitcast(BF16)]
              Vt = [SCR[:, 512 + 96 * i:512 + 96 * (i + 1)].bitcast(BF16) for i in range(3)]
              for i in range(3):
                  P.op(G, (lambda i: lambda e: e.memset(Vt[i][:, 64:128], 1.0))(i), writes=[("Vt", i)])
              accbanks = [0, 1, 4]
              cn = [0]
              for hp in range(4):
                  slot = hp % 2
                  if hp < 3:
                      load_wblock(whpd[hp + 1], (hp + 1) % 2, 512)
                  else:
                      load_wblock(woutd[:, :, 0:512], 0, 512, gain=False)
                  pending = []

                  def rope_unit(u):
                      pp, kraw, t1, kk, tk, t0, dst, ch, which = u
                      P.op(T, lambda e: e.matmul(pp, lhsT=perm_b, rhs=kraw, start=True, stop=True), reads=["cb", kk], writes=[ps_key(pp)])
                      P.op(V, lambda e: e.tensor_tensor(out=t1, in0=pp, in1=ropeS[:, t0:t0 + 512], op=ALU.mult), reads=[ps_key(pp), "ropeS"], writes=[tk])
                      P.op(V, lambda e: e.tensor_tensor(out=kraw, in0=kraw, in1=ropeC[:, t0:t0 + 512], op=ALU.mult), reads=[kk, "ropeC"], writes=[kk])
                      P.op(V, lambda e: e.tensor_tensor(out=dst[:, ch * 512:(ch + 1) * 512], in0=kraw, in1=t1, op=ALU.add),
                           reads=[kk, tk], writes=["kq%d" % which])

                  zc = set(t // 4 for t in zero_window_tiles(job))
                  for ch in sorted(zc):
                      P.op(G, (lambda ch: lambda e: e.memset(kT[:, ch * 512:(ch + 1) * 512], 0.0))(ch), writes=["kq0"])
                      P.op(G, (lambda ch: lambda e: e.memset(vT[:, ch * 512:(ch + 1) * 512], 0.0))(ch), writes=["vT"])
                  for which, (c0, tbase, nch, dst) in enumerate(((128, 0, NE // 512, kT), (0, HALO, NQ // 512, qT))):
                      for ch in range(nch):
                          if which == 0 and ch in zc:
                              continue
                          t0 = tbase + ch * 512
                          n_ = cn[0]
                          cn[0] += 1
                          kraw = kraws[n_ % 2]
                          t1 = t1s[n_ % 2]
                          kk = ("kraw", n_ % 2)
                          tk = ("t1", n_ % 2)
                          pk = pv(accbanks[n_ % 3], 0, 512)
                          proj(pk, slot, c0, t0, 512)
                          P.op(A, (lambda pk, kraw: lambda e: e.activation(out=kraw, in_=pk, func=AF.Copy))(pk, kraw), reads=[ps_key(pk)], writes=[kk])
                          pp = pv(2 + n_ % 2, 0, 512)
                          if pending:
                              rope_unit(pending.pop(0))
                          pending.append((pp, kraw, t1, kk, tk, t0, dst, ch, which))
                  for ch in range(NE // 512):
                      if ch in zc:
                          continue
                      n_ = cn[0]
                      cn[0] += 1
                      pv_ = pv(accbanks[n_ % 3], 0, 512)
                      proj(pv_, slot, 256, ch * 512, 512)
                      if pending:
                          rope_unit(pending.pop(0))
                      if ch % 2 == 0:
                          P.op(A, (lambda pv_, ch: lambda e: e.activation(out=vT[:, ch * 512:(ch + 1) * 512], in_=pv_, func=AF.Copy))(pv_, ch),
                               reads=[ps_key(pv_)], writes=["vT"])
                      else:
                          P.op(V, (lambda pv_, ch: lambda e: e.tensor_copy(out=vT[:, ch * 512:(ch + 1) * 512], in_=pv_))(pv_, ch),
                               reads=[ps_key(pv_)], writes=["vT"])
                  for ch in range(NQ // 512):
                      n_ = cn[0]
                      cn[0] += 1
                      pg = pv(accbanks[n_ % 3], 0, 512)
                      proj(pg, slot, 384, HALO + ch * 512, 512)
                      P.op(A, (lambda pg, ch: lambda e: e.activation(out=sgT[:, ch * 512:(ch + 1) * 512], in_=pg, func=AF.Silu))(pg, ch),
                           reads=[ps_key(pg)], writes=["sgT"])
                  if hp == 3:
                      load_wblock(woutd[:, :, 512:1024], 1, 512, gain=False)
                  stage_end("Dp")

                  tasks = []
                  kt_base = 0
                  for pi_, d in enumerate(PATTERNS):
                      nt = NQ // (128 * d)
                      for r in range(d):
                          for i in range(nt + 1):
                              tasks.append(dict(d=d, r=r, i=i, nt=nt, kti=kt_base + r * (nt + 1) + i, first=(pi_ == 0)))
                      kt_base += d * (nt + 1)
                  if ktlimit is not None:
                      tasks = tasks[:ktlimit]

                  def geom(tk_):
                      d, r, i, nt = tk_["d"], tk_["r"], tk_["i"], tk_["nt"]
                      base = HALO // d
                      m0 = base + 128 * i
                      jlo = max(i - 1, 0)
                      jhi = min(i, nt - 1)
                      qa = (jlo - (i - 1)) * 128
                      qb = (jhi - (i - 1) + 1) * 128
                      mq0 = (m0 - 128 + qa) - base
                      return d, r, i, nt, m0, jlo, jhi, qa, qb, mq0

                  def ph1(n, tk_):
                      d, r, i, nt, m0, jlo, jhi, qa, qb, mq0 = geom(tk_)
                      vs = n % 3
                      kcols = kT.rearrange("p (m d) -> p m d", d=d)[:, m0 - 64:m0 + 64, r]
                      vcols = vT.rearrange("p (m d) -> p m d", d=d)[:, m0 - 64:m0 + 64, r]
                      qcols = qT.rearrange("p (m d) -> p m d", d=d)[:, mq0:mq0 + (qb - qa), r]
                      ptx = PT16s[n % 2]
                      ptk = PT16k[n % 2]
                      P.op(T, lambda e: e.transpose(ptx[:, 0:128], vcols, ident_b), reads=["vT", "cb"], writes=[ptk])
                      if n % 2 == 0:
                          P.op(A, lambda e: e.activation(out=Vt[vs].rearrange("p (a b) -> p a b", b=64)[:, 0:3:2, :],
                                                         in_=ptx[:, 0:128].rearrange("p (a b) -> p a b", b=64), func=AF.Copy),
                               reads=[ptk], writes=[("Vt", vs)])
                      else:
                          P.op(V, lambda e: e.tensor_copy(out=Vt[vs].rearrange("p (a b) -> p a b", b=64)[:, 0:3:2, :],
                                                          in_=ptx[:, 0:128].rearrange("p (a b) -> p a b", b=64)),
                               reads=[ptk], writes=[("Vt", vs)])
                      sbank = 2 * (n % 2)
                      for h in range(2):
                          psh = PB[sbank + h][:, qa:qb]
                          P.op(T, (lambda psh, h: lambda e: e.matmul(psh, lhsT=kcols[64 * h:64 * h + 64], rhs=qcols[64 * h:64 * h + 64],
                                                                     start=True, stop=True))(psh, h),
                               reads=["kq0", "kq1"], writes=[("pb", sbank + h)])

                  def ph2(n, tk_):
                      d, r, i, nt, m0, jlo, jhi, qa, qb, mq0 = geom(tk_)
                      kti = tk_["kti"]
                      sbank = 2 * (n % 2)
                      ptt = PTt[n % 2].rearrange("p (h q) -> p h q", h=2)
                      psv = PSALL[:, 512 * sbank:512 * (sbank + 2)].rearrange("p (h q) -> p h q", h=2)
                      P.op(A, lambda e: e.activation(out=ptt[:, :, qa:qb], in_=psv[:, :, qa:qb], func=AF.Exp, bias=vbt[:, kti:kti + 1], scale=0.125),
                           reads=[("pb", sbank), ("pb", sbank + 1), "vbt"], writes=[("PT", n % 2)])
                      P.op(V, lambda e: e.tensor_tensor(
                          out=ptt[:, :, qa:qb], in0=ptt[:, :, qa:qb],
                          in1=band_b[:, qa:qb].unsqueeze(1).to_broadcast([128, 2, qb - qa]), op=ALU.mult),
                          reads=[("PT", n % 2), "cb"], writes=[("PT", n % 2)])

                  def ph3(n, tk_):
                      d, r, i, nt, m0, jlo, jhi, qa, qb, mq0 = geom(tk_)
                      vs = n % 3
                      first = tk_["first"]
                      ptt = PTt[n % 2].rearrange("p (h q) -> p h q", h=2)
                      merged = (d != 16) and (jlo == i - 1) and (jhi == i) and (jlo % 4 != 3)
                      if merged:
                          sl0 = jlo % 4
                          for h in range(2):
                              po = PB[5 + h][:, sl0 * 128:(sl0 + 2) * 128]
                              lw = Vt[vs][:, 0:128] if h == 0 else Vt[vs][:, 64:192]
                              P.op(T, (lambda po, lw, h: lambda e: e.matmul(po, lhsT=lw, rhs=ptt[:, h, 0:256], start=False, stop=True,
                                                                            skip_group_check=True))(po, lw, h),
                                   reads=[("Vt", vs), ("PT", n % 2)], writes=[("po", h, sl0), ("po", h, sl0 + 1)])
                      for j in range(jlo, jhi + 1):
                          ca = (j - (i - 1)) * 128
                          sl = (r % 4) if d == 16 else (j % 4)
                          if not merged:
                              for h in range(2):
                                  po = PB[5 + h][:, sl * 128:(sl + 1) * 128]
                                  lw = Vt[vs][:, 0:128] if h == 0 else Vt[vs][:, 64:192]
                                  st_flag = (j == i) if d == 16 else (j == i and sl == 0)
                                  P.op(T, (lambda po, lw, h, ca, j, st_flag: lambda e: e.matmul(po, lhsT=lw, rhs=ptt[:, h, ca:ca + 128],
                                                                                                start=st_flag, stop=(j == i - 1),
                                                                                                skip_group_check=True))(po, lw, h, ca, j, st_flag),
                                       reads=[("Vt", vs), ("PT", n % 2)], writes=[("po", h, sl)])
                          if j == i - 1:
                              keys4 = lambda h: [("po", h, s_) for s_ in range(4)]
                              if d == 16:
                                  if r % 4 == 3:
                                      grp = r // 4
                                      for h in range(2):
                                          acc = accA if h == 0 else accB
                                          src = PB[5 + h].rearrange("p (a m) -> p a m", a=4)
                                          dstv = acc.rearrange("p (m x) -> p m x", x=16)[:, :, 4 * grp:4 * grp + 4].rearrange("p m a -> p a m")
                                          ev(P, first, dstv, src, h, keys4(h))
                              elif j % 4 == 3:
                                  jg = j // 4
                                  for h in range(2):
                                      acc = accA if h == 0 else accB
                                      src = PB[5 + h].rearrange("p (a m) -> p a m", a=4)
                                      if d == 4:
                                          dstv = acc.rearrange("p (m x) -> p m x", x=4)[:, :, r].rearrange("p (a m) -> p a m", a=4)
                                      else:
                                          dstv = acc[:, jg * 512:(jg + 1) * 512].rearrange("p (a m) -> p a m", a=4)
                                      ev(P, first, dstv, src, h, keys4(h))

                  NTK = len(tasks)
                  for n in range(NTK + 2):
                      if 1 <= n <= NTK:
                          ph2(n - 1, tasks[n - 1])
                      if n < NTK:
                          ph1(n, tasks[n])
                      if n >= 2:
                          ph3(n - 2, tasks[n - 2])
                  stage_end("Da")
                  P.op(A, lambda e: e.activation(out=Dt[0:64, :], in_=accA[64:128, :], func=AF.Copy), reads=["accA"], writes=["Dt"])
                  P.op(A, lambda e: e.activation(out=Dt[64:128, :], in_=accB[0:64, :], func=AF.Copy), reads=["accB"], writes=["Dt"])
                  P.op(A, lambda e: e.activation(out=Dt, in_=Dt, func=AF.Ln), reads=["Dt"], writes=["Dt"])
                  P.op(A, lambda e: e.activation(out=Dt, in_=Dt, func=AF.Exp, scale=-1.0), reads=["Dt"], writes=["Dt"])
                  P.op(V, lambda e: e.tensor_tensor(out=Dt, in0=Dt, in1=sgT, op=ALU.mult), reads=["Dt", "sgT"], writes=["Dt"])
                  P.op(V, (lambda hp: lambda e: e.tensor_tensor(out=catT[0:64, hp, :], in0=accA[0:64, :], in1=Dt[0:64, :], op=ALU.mult))(hp),
                       reads=["accA", "Dt"], writes=["catT"])
                  P.op(V, (lambda hp: lambda e: e.tensor_tensor(out=catT[64:128, hp, :], in0=accB[64:128, :], in1=Dt[64:128, :], op=ALU.mult))(hp),
                       reads=["accB", "Dt"], writes=["catT"])
                  stage_end("Df")
              P.barrier(dummy)
              stage_end("D")

              e_steps = make_E(job)
              if job + 1 < njob:
                  A_cur = make_A(job + 1)
                  a_steps = list(A_cur["steps"])
              else:
                  a_steps = []
              for st_ in e_steps:
                  st_()
                  if a_steps:
                      a_steps.pop(0)()
              for st_ in a_steps:
                  st_()
        except _Stop:
            pass

        P.wait_all(S)
        P.emit(st)
    return nc


def ev(P, first, dstv, src, h, keys):
    key = "accA" if h == 0 else "accB"
    if first:
        P.op("vector", lambda e: e.tensor_copy(out=dstv, in_=src), reads=keys, writes=[key])
    else:
        P.op("vector", lambda e: e.tensor_tensor(out=dstv, in0=dstv, in1=src, op=ALU.add), reads=keys + [key], writes=[key])


def host_consts():
    cm = np.zeros((128, 640), np.float32)
    cm[:, 0:128] = np.eye(128, dtype=np.float32)
    perm = np.zeros((128, 128), np.float32)
    for m in range(128):
        c = m % 64
        if c < 8:
            perm[m + 8, m] = 1.0
        elif c < 16:
            perm[m - 8, m] = 1.0
    cm[:, 128:256] = perm
    p = np.arange(128)[:, None]
    j = np.arange(256)[None, :]
    cm[:, 256:512] = ((j >= p) & (j <= p + 128)).astype(np.float32)
    cm[:, 512:640] = 1.0 / 512.0
    return cm


def rope_consts():
    invf = np.zeros(128, np.float32)
    sgn = np.zeros(128, np.float32)
    inv = (500000.0 ** (-np.arange(8, dtype=np.float32) * 2.0 / 16.0)).astype(np.float32)
    for m in range(128):
        c = m % 64
        if c < 16:
            invf[m] = inv[c % 8]
            sgn[m] = -1.0 if c < 8 else 1.0
    return invf, sgn


def job_meta(seq_len, q_start):
    wpos = np.arange(q_start - HALO, q_start - HALO + NE)
    valid = (wpos >= 0) & (wpos < seq_len)
    pos = np.where(valid, wpos, 0).astype(np.float32)
    kv = np.zeros((128, NKT), np.float32)
    pidx = np.arange(128)
    for n, (d, r, i, nt) in enumerate(KT_LIST):
        base = HALO // d
        m = base - 64 + 128 * i + pidx
        e = d * m + r
        kv[:, n] = valid[e].astype(np.float32)
    return pos, kv, valid


def make_inputs(x_prompt, x_sample, norm_pre, w_in, conv_w, conv_b, conv_ln_g, conv_ln_b, w_pw, b_pw, w_out, norm_post, njob=NJOB):
    f = np.float32
    w_in = np.asarray(w_in, f)[0]
    cm = host_consts()
    invf, sgn = rope_consts()
    prm = np.zeros((128, 160), f)
    prm[:, 0:8] = np.asarray(norm_pre, f)[0].reshape(8, 128).T
    prm[:, 8:12] = np.asarray(conv_b, f)[0].reshape(4, 128).T
    prm[:, 12:16] = np.asarray(conv_ln_g, f)[0].reshape(4, 128).T
    prm[:, 16:20] = np.asarray(conv_ln_b, f)[0].reshape(4, 128).T
    prm[:, 20:24] = np.asarray(b_pw, f)[0].reshape(4, 128).T
    cw = np.asarray(conv_w, f)[0]
    prm[:, 24:148] = cw.T.reshape(4, 128, 31).transpose(1, 0, 2).reshape(128, 124)
    prm[:, 148] = invf
    prm[:, 149] = sgn
    prm[:, 150] = (invf.astype(np.float64) / (2.0 * np.pi)).astype(f)
    prm[:, 151] = (sgn.astype(np.float64) * 2.0 * np.pi).astype(f)
    gpost = np.ascontiguousarray(np.broadcast_to(np.asarray(norm_post, f)[0][None, :], (128, D)))
    wk = w_in.reshape(8, 128, 3584).transpose(1, 0, 2)
    wcv = np.stack([np.concatenate([wk[:, :, 2048 + g * 128:2048 + (g + 1) * 128],
                                    wk[:, :, 2560 + g * 128:2560 + (g + 1) * 128],
                                    wk[:, :, 3072 + g * 128:3072 + (g + 1) * 128]], axis=2) for g in range(4)])
    whp = np.stack([np.concatenate([wk[:, :, 0 + h * 128:0 + (h + 1) * 128],
                                    wk[:, :, 512 + h * 128:512 + (h + 1) * 128],
                                    wk[:, :, 1024 + h * 128:1024 + (h + 1) * 128],
                                    wk[:, :, 1536 + h * 128:1536 + (h + 1) * 128]], axis=2) for h in range(4)])
    wpw = np.ascontiguousarray(np.asarray(w_pw, f)[0].reshape(4, 128, 512).transpose(1, 0, 2))
    wout = np.ascontiguousarray(np.asarray(w_out, f)[0].reshape(8, 128, 1024).transpose(1, 0, 2))
    shared = {"cmat": cm, "prm": prm, "gpost": gpost, "wcv": np.ascontiguousarray(wcv), "whp": np.ascontiguousarray(whp),
              "wpw": wpw, "wout": wout}
    xp = np.asarray(x_prompt, f)
    xs = np.asarray(x_sample, f)[0]
    S_P = xp.shape[1]
    S_S = xs.shape[0]
    in_maps = []
    for c in range(NCORE):
        xw = np.zeros((njob, NE, D), f)
        pos = np.zeros((njob, NE), f)
        kval = np.zeros((njob, 128, NKT), f)
        jobs = [("p", c, 0), ("p", c, NQ), ("s", 0, c * NQ)][:njob]
        for jn, (kind, b, q0) in enumerate(jobs):
            src = xp[b] if kind == "p" else xs
            slen = S_P if kind == "p" else S_S
            lo = q0 - HALO
            hi = lo + NE
            a = max(lo, 0)
            bnd = min(hi, slen)
            xw[jn, a - lo:bnd - lo] = src[a:bnd]
            p_, kv_, _ = job_meta(slen, q0)
            pos[jn] = p_
            kval[jn] = kv_
        m = {"xw": xw, "pos": pos, "kval": kval}
        m.update(shared)
        in_maps.append(m)
    return in_maps


_NC_CACHE = {}


def kernel(x_prompt, x_sample, norm_pre, w_in, conv_w, conv_b, conv_ln_g, conv_ln_b, w_pw, b_pw, w_out, norm_post):
    in_maps = make_inputs(x_prompt, x_sample, norm_pre, w_in, conv_w, conv_b, conv_ln_g, conv_ln_b, w_pw, b_pw, w_out, norm_post)
    if "nc" not in _NC_CACHE:
        _NC_CACHE["nc"] = build_program()
    nc = _NC_CACHE["nc"]
    res = run_bass_kernel_spmd(nc, in_maps, core_ids=list(range(NCORE)))
    B, SP = np.asarray(x_prompt).shape[0:2]
    SS = np.asarray(x_sample).shape[1]
    y_prompt = np.zeros((B, SP, D), np.float32)
    y_sample = np.zeros((1, SS, D), np.float32)
    for c in range(NCORE):
        y = res.results[c]["y"]
        y_prompt[c, 0:NQ] = y[0]
        y_prompt[c, NQ:2 * NQ] = y[1]
        y_sample[0, c * NQ:(c + 1) * NQ] = y[2]
    return (y_prompt, y_sample)
```
